# Optimizing a Trainium2 kernel written in Bass

```python
import math
import jax, jax.numpy as jnp
from jax import lax
import numpy as np


D_MODEL = 4096
BATCH = 2
SEQ = 8192
DEPTH = 1

GRID_W = 64
CTX_LEN = 256
MIX_WIDTH = D_MODEL
LRU_WIDTH = MIX_WIDTH // 2
LRU_BLOCKS = 16
LRU_BLOCK_DIM = LRU_WIDTH // LRU_BLOCKS
LRU_C = 8.0
GDN_WIDTH = MIX_WIDTH - LRU_WIDTH
GDN_HEADS = 16
GDN_HEAD_DIM = GDN_WIDTH // GDN_HEADS
GDN_CHUNK = 64
CONV_W = 4
CONV_PAD_L = 2
CONV_PAD_R = 1
N_DIR = 2
D_FF = -(-(8 * D_MODEL) // (3 * 256)) * 256
N_MOD = 6
EPS = 1e-6
COL_LRU_X = 0
COL_LRU_GATE = COL_LRU_X + LRU_WIDTH
COL_QKV = COL_LRU_GATE + LRU_WIDTH
COL_Z = COL_QKV + 3 * GDN_WIDTH
COL_BETA = COL_Z + GDN_WIDTH
COL_ALPHA = COL_BETA + N_DIR * GDN_HEADS
N_IN = COL_ALPHA + N_DIR * GDN_HEADS

kernel_name = 'hybrid_rglru_gdn_dit_layer'


def _rmsnorm(x, g):
    xf = x.astype(jnp.float32)
    y = xf * lax.rsqrt(jnp.mean(xf * xf, axis=-1, keepdims=True) + EPS)
    return (y * g.astype(jnp.float32)).astype(x.dtype)


def _l2norm(x):
    xf = x.astype(jnp.float32)
    return xf * lax.rsqrt(jnp.sum(xf * xf, axis=-1, keepdims=True) + EPS)


def _modulate(h, shift, scale):
    return h * (1.0 + scale) + shift


def _cols(p, start, width):
    return p[..., start:start + width]


def _dwconv_centred(x, w):
    t = x.shape[1]
    xp = jnp.pad(x, ((0, 0), (CONV_PAD_L, CONV_PAD_R), (0, 0)))
    y = xp[:, 0:t] * w[0]
    for j in range(1, CONV_W):
        y = y + xp[:, j:j + t] * w[j]
    return y


def _raster_to_colmajor(t, rows):
    b, n = t.shape[:2]
    rest = t.shape[2:]
    return t.reshape((b, rows, GRID_W) + rest).swapaxes(1, 2).reshape((b, n) + rest)


def _colmajor_to_raster(t, rows):
    b, n = t.shape[:2]
    rest = t.shape[2:]
    return t.reshape((b, GRID_W, rows) + rest).swapaxes(1, 2).reshape((b, n) + rest)


def _flip(t, rev):
    return t[:, ::-1] if rev else t


def _lin_combine(e1, e2):
    a1, b1 = e1
    a2, b2 = e2
    return a1 * a2, a2 * b1 + b2


def _linear_scan(a, b, h0, rev):
    first = -1 if rev else 0
    if h0 is not None:
        b = b.at[:, first].add(a[:, first] * h0)
    _, h = lax.associative_scan(_lin_combine, (a, b), reverse=rev, axis=1)
    final = h[:, 0] if rev else h[:, -1]
    return h, final


def _rglru_coeffs(xc, w_a, b_a, w_x, b_x, lam):
    bsz, n, _ = xc.shape
    xb = xc.reshape(bsz, n, LRU_BLOCKS, LRU_BLOCK_DIM)
    r = jax.nn.sigmoid(jnp.einsum('btni,nij->btnj', xb, w_a).reshape(bsz, n, LRU_WIDTH) + b_a)
    i = jax.nn.sigmoid(jnp.einsum('btni,nij->btnj', xb, w_x).reshape(bsz, n, LRU_WIDTH) + b_x)
    log_a = -LRU_C * r * jax.nn.softplus(-lam)
    a = jnp.exp(log_a)
    return a, jnp.sqrt(-jnp.expm1(2.0 * log_a)) * (i * xc)


def _rglru_mixer(u_lat, gate_lat, u_ctx, gate_ctx, conv_w, conv_b, w_a, b_a, w_x, b_x, lam, norm_g, with_ctx):
    xl = (_dwconv_centred(u_lat, conv_w) + conv_b).astype(jnp.float32)
    xc = (_dwconv_centred(u_ctx, conv_w) + conv_b).astype(jnp.float32)
    h_lat, h_ctx = [], []
    for d in range(N_DIR):
        rev = d == 1
        a_c, b_c = _rglru_coeffs(xc, w_a[d], b_a[d], w_x[d], b_x[d], lam[d])
        hc, hc_final = _linear_scan(a_c, b_c, None, rev)
        a_l, b_l = _rglru_coeffs(xl, w_a[d], b_a[d], w_x[d], b_x[d], lam[d])
        hl, _ = _linear_scan(a_l, b_l, hc_final, rev)
        h_lat.append(hl)
        h_ctx.append(hc)
    y_lat = _rmsnorm(((h_lat[0] + h_lat[1]) * jax.nn.gelu(gate_lat.astype(jnp.float32))).astype(u_lat.dtype), norm_g)
    y_ctx = None
    if with_ctx:
        y_ctx = _rmsnorm(((h_ctx[0] + h_ctx[1]) * jax.nn.gelu(gate_ctx.astype(jnp.float32))).astype(u_ctx.dtype), norm_g)
    return y_lat, y_ctx


def _to_chunks(t):
    b, n, h = t.shape[:3]
    rest = t.shape[3:]
    t = t.reshape((b, n // GDN_CHUNK, GDN_CHUNK, h) + rest)
    return jnp.moveaxis(t, (1, 3), (0, 2))


def _gdn_chunked(q, k, v, g, beta, s0, need_out):
    bsz, n, h, dk = q.shape
    dv = v.shape[-1]
    q, k, v, g, beta = (_to_chunks(t.astype(jnp.float32)) for t in (q, k, v, g, beta))
    q = q * (dk ** -0.5)
    gcum = jnp.cumsum(g, axis=-1)
    idx = jnp.arange(GDN_CHUNK)
    lower = idx[:, None] >= idx[None, :]
    strict = idx[:, None] > idx[None, :]
    diff = gcum[..., :, None] - gcum[..., None, :]
    decay = jnp.where(lower, jnp.exp(jnp.where(lower, diff, 0.0)), 0.0)
    kb = k * beta[..., None]
    lmat = jnp.where(strict, jnp.einsum('nbhid,nbhjd->nbhij', kb, k) * decay, 0.0)
    amat = lmat + jnp.eye(GDN_CHUNK, dtype=jnp.float32)
    rhs = jnp.concatenate([v * beta[..., None], kb * jnp.exp(gcum)[..., None]], axis=-1)
    sol = lax.linalg.triangular_solve(amat, rhs, left_side=True, lower=True, unit_diagonal=True)
    u, w = sol[..., :dv], sol[..., dv:]
    k_tail = k * jnp.exp(gcum[..., -1:] - gcum)[..., None]
    g_last = jnp.exp(gcum[..., -1])
    if s0 is None:
        s0 = jnp.zeros((bsz, h, dk, dv), jnp.float32)
    if need_out:
        intra = jnp.where(lower, jnp.einsum('nbhid,nbhjd->nbhij', q, k) * decay, 0.0)
        xs = (w, u, k_tail, g_last, q * jnp.exp(gcum)[..., None], intra)
    else:
        xs = (w, u, k_tail, g_last)

    def step(s, xs_i):
        w_i, u_i, kt_i, gl_i = xs_i[:4]
        v_new = u_i - jnp.einsum('bhck,bhkv->bhcv', w_i, s)
        s_new = s * gl_i[..., None, None] + jnp.einsum('bhck,bhcv->bhkv', kt_i, v_new)
        if not need_out:
            return s_new, None
        qd_i, intra_i = xs_i[4:]
        o_i = jnp.einsum('bhck,bhkv->bhcv', qd_i, s) + jnp.einsum('bhij,bhjv->bhiv', intra_i, v_new)
        return s_new, o_i

    s_final, o = lax.scan(step, s0.astype(jnp.float32), xs)
    if need_out:
        o = jnp.moveaxis(o, (0, 2), (1, 3)).reshape(bsz, n, h, dv)
    return o, s_final


def _gdn_qkv(qkv, conv_w):
    bsz, n, _ = qkv.shape
    y = jax.nn.silu(_dwconv_centred(qkv, conv_w))
    q, k, v = jnp.split(y, 3, axis=-1)
    shp = (bsz, n, GDN_HEADS, GDN_HEAD_DIM)
    return _l2norm(q.reshape(shp)), _l2norm(k.reshape(shp)), v.reshape(shp)


def _gdn_gates(p_beta, p_alpha, a_log_d, dt_bias_d):
    beta = jax.nn.sigmoid(p_beta.astype(jnp.float32))
    g = -jnp.exp(a_log_d.astype(jnp.float32)) * jax.nn.softplus(p_alpha.astype(jnp.float32) + dt_bias_d)
    return beta, g


def _gated_headnorm(o, z, g):
    bsz, n = z.shape[:2]
    zf = z.reshape(bsz, n, GDN_HEADS, GDN_HEAD_DIM).astype(jnp.float32)
    y = _rmsnorm(o, g) * jax.nn.silu(zf)
    return y.reshape(bsz, n, GDN_WIDTH).astype(z.dtype)


def _gdn_mixer(qkv_lat, beta_lat, alpha_lat, z_lat, qkv_ctx, beta_ctx, alpha_ctx, z_ctx,
               conv_w, a_log, dt_bias, norm_g, rows, with_ctx):
    qkv_lat, beta_lat, alpha_lat = (_raster_to_colmajor(t, rows) for t in (qkv_lat, beta_lat, alpha_lat))
    lat = _gdn_qkv(qkv_lat, conv_w)
    cqkv = _gdn_qkv(qkv_ctx, conv_w)
    o_lat, o_ctx = [], []
    for d in range(N_DIR):
        rev = d == 1
        hs = slice(d * GDN_HEADS, (d + 1) * GDN_HEADS)
        b_c, g_c = _gdn_gates(beta_ctx[..., hs], alpha_ctx[..., hs], a_log[d], dt_bias[d])
        b_l, g_l = _gdn_gates(beta_lat[..., hs], alpha_lat[..., hs], a_log[d], dt_bias[d])
        oc, s_ctx = _gdn_chunked(*(_flip(t, rev) for t in cqkv + (g_c, b_c)), None, with_ctx)
        ol, _ = _gdn_chunked(*(_flip(t, rev) for t in lat + (g_l, b_l)), s_ctx, True)
        o_lat.append(_flip(ol, rev))
        if with_ctx:
            o_ctx.append(_flip(oc, rev))
    y_lat = _gated_headnorm(_colmajor_to_raster(o_lat[0] + o_lat[1], rows), z_lat, norm_g)
    y_ctx = _gated_headnorm(o_ctx[0] + o_ctx[1], z_ctx, norm_g) if with_ctx else None
    return y_lat, y_ctx


def _swiglu(h, wg, wu, wd):
    return (jax.nn.silu(h @ wg) * (h @ wu)) @ wd


def setup_inputs(seed: int = 0) -> dict:
    key = jax.random.key(seed)
    ks = jax.random.split(key, 32)
    L = DEPTH
    f32 = jnp.float32

    def nrm(k, shape, scale):
        return jax.random.normal(k, shape, f32) * scale

    def gain(k, shape):
        return 1.0 + 0.05 * jax.random.normal(k, shape, f32)

    u = jax.random.uniform(ks[17], (L, N_DIR, LRU_WIDTH), f32, minval=0.9, maxval=0.999)
    s = u ** (1.0 / LRU_C)
    lru_lambda = jnp.log(s) - jnp.log1p(-s)
    gdn_a_log = jnp.log(jax.random.uniform(ks[20], (L, N_DIR, GDN_HEADS), f32, minval=1.0, maxval=16.0))
    dt = jnp.exp(jax.random.uniform(ks[21], (L, N_DIR, GDN_HEADS), f32,
                                    minval=math.log(1e-3), maxval=math.log(1e-1)))
    gdn_dt_bias = dt + jnp.log(-jnp.expm1(-dt))
    return {
        'x': nrm(ks[0], (BATCH, SEQ, D_MODEL), 1.0),
        'c': nrm(ks[1], (BATCH, D_MODEL), 1.0),
        'ctx': nrm(ks[2], (BATCH, CTX_LEN, D_MODEL), 1.0),
        'c_ctx': nrm(ks[3], (D_MODEL,), 1.0),
        'w_ada': nrm(ks[4], (L, D_MODEL, N_MOD * D_MODEL), 0.5 * D_MODEL ** -0.5),
        'b_ada': nrm(ks[5], (L, N_MOD * D_MODEL), 0.01),
        'g_pre_mix': gain(ks[6], (L, D_MODEL)),
        'g_post_mix': gain(ks[7], (L, D_MODEL)),
        'g_pre_ffn': gain(ks[8], (L, D_MODEL)),
        'g_post_ffn': gain(ks[9], (L, D_MODEL)),
        'w_in': nrm(ks[10], (L, D_MODEL, N_IN), D_MODEL ** -0.5),
        'lru_conv_w': nrm(ks[11], (L, CONV_W, LRU_WIDTH), CONV_W ** -0.5),
        'lru_conv_b': nrm(ks[12], (L, LRU_WIDTH), 0.01),
        'lru_w_a': nrm(ks[13], (L, N_DIR, LRU_BLOCKS, LRU_BLOCK_DIM, LRU_BLOCK_DIM), LRU_BLOCK_DIM ** -0.5),
        'lru_b_a': nrm(ks[14], (L, N_DIR, LRU_WIDTH), 0.01),
        'lru_w_x': nrm(ks[15], (L, N_DIR, LRU_BLOCKS, LRU_BLOCK_DIM, LRU_BLOCK_DIM), LRU_BLOCK_DIM ** -0.5),
        'lru_b_x': nrm(ks[16], (L, N_DIR, LRU_WIDTH), 0.01),
        'lru_lambda': lru_lambda,
        'lru_norm_g': gain(ks[18], (L, LRU_WIDTH)),
        'gdn_conv_w': nrm(ks[19], (L, CONV_W, 3 * GDN_WIDTH), CONV_W ** -0.5),
        'gdn_a_log': gdn_a_log,
        'gdn_dt_bias': gdn_dt_bias,
        'gdn_norm_g': gain(ks[22], (L, GDN_HEAD_DIM)),
        'w_out': nrm(ks[23], (L, MIX_WIDTH, D_MODEL), MIX_WIDTH ** -0.5),
        'w_ffn_gate': nrm(ks[24], (L, D_MODEL, D_FF), D_MODEL ** -0.5),
        'w_ffn_up': nrm(ks[25], (L, D_MODEL, D_FF), D_MODEL ** -0.5),
        'w_ffn_down': nrm(ks[26], (L, D_FF, D_MODEL), D_FF ** -0.5),
    }


def reference(x, c, ctx, c_ctx, w_ada, b_ada, g_pre_mix, g_post_mix, g_pre_ffn, g_post_ffn, w_in,
              lru_conv_w, lru_conv_b, lru_w_a, lru_b_a, lru_w_x, lru_b_x, lru_lambda, lru_norm_g,
              gdn_conv_w, gdn_a_log, gdn_dt_bias, gdn_norm_g, w_out, w_ffn_gate, w_ffn_up, w_ffn_down):
    rows = x.shape[1] // GRID_W
    for l in range(DEPTH):
        with_ctx = l < DEPTH - 1
        shift_m, scale_m, gate_m, shift_f, scale_f, gate_f = jnp.split(
            (jax.nn.silu(c) @ w_ada[l] + b_ada[l])[:, None, :], N_MOD, axis=-1)
        mod_ctx = jnp.split((jax.nn.silu(c_ctx) @ w_ada[l] + b_ada[l])[None, None, :], N_MOD, axis=-1)

        h_lat = _modulate(_rmsnorm(x, g_pre_mix[l]), shift_m, scale_m)
        h_ctx = _modulate(_rmsnorm(ctx, g_pre_mix[l]), mod_ctx[0], mod_ctx[1])
        p_lat = h_lat @ w_in[l]
        p_ctx = h_ctx @ w_in[l]
        lru_lat, lru_ctx = _rglru_mixer(
            _cols(p_lat, COL_LRU_X, LRU_WIDTH), _cols(p_lat, COL_LRU_GATE, LRU_WIDTH),
            _cols(p_ctx, COL_LRU_X, LRU_WIDTH), _cols(p_ctx, COL_LRU_GATE, LRU_WIDTH),
            lru_conv_w[l], lru_conv_b[l], lru_w_a[l], lru_b_a[l], lru_w_x[l], lru_b_x[l],
            lru_lambda[l], lru_norm_g[l], with_ctx)
        gdn_lat, gdn_ctx = _gdn_mixer(
            _cols(p_lat, COL_QKV, 3 * GDN_WIDTH), _cols(p_lat, COL_BETA, N_DIR * GDN_HEADS),
            _cols(p_lat, COL_ALPHA, N_DIR * GDN_HEADS), _cols(p_lat, COL_Z, GDN_WIDTH),
            _cols(p_ctx, COL_QKV, 3 * GDN_WIDTH), _cols(p_ctx, COL_BETA, N_DIR * GDN_HEADS),
            _cols(p_ctx, COL_ALPHA, N_DIR * GDN_HEADS), _cols(p_ctx, COL_Z, GDN_WIDTH),
            gdn_conv_w[l], gdn_a_log[l], gdn_dt_bias[l], gdn_norm_g[l], rows, with_ctx)
        mix_lat = jnp.concatenate([lru_lat, gdn_lat], axis=-1)
        x = x + gate_m * _rmsnorm(mix_lat @ w_out[l], g_post_mix[l])

        h = _modulate(_rmsnorm(x, g_pre_ffn[l]), shift_f, scale_f)
        x = x + gate_f * _rmsnorm(_swiglu(h, w_ffn_gate[l], w_ffn_up[l], w_ffn_down[l]), g_post_ffn[l])

        if with_ctx:
            mix_ctx = jnp.concatenate([lru_ctx, gdn_ctx], axis=-1)
            ctx = ctx + mod_ctx[2] * _rmsnorm(mix_ctx @ w_out[l], g_post_mix[l])
            hc = _modulate(_rmsnorm(ctx, g_pre_ffn[l]), mod_ctx[3], mod_ctx[4])
            ctx = ctx + mod_ctx[5] * _rmsnorm(_swiglu(hc, w_ffn_gate[l], w_ffn_up[l], w_ffn_down[l]), g_post_ffn[l])
    return x
```

```python
import contextlib
import numpy as np
import concourse.bass as bass
import concourse.mybir as mybir
from concourse.bass_utils import run_bass_kernel_spmd

F32 = mybir.dt.float32
BF16 = mybir.dt.bfloat16
AF = mybir.ActivationFunctionType
ALU = mybir.AluOpType

D = 4096
SEQ = 8192
CTX = 256
NT = SEQ + CTX
NIN = 12352
DFF = 11008
EPS = 1e-6
COL_G = 2048
COL_QKV = 4096
COL_Z = 4096 + 6144
COL_BETA = COL_Z + 2048
COL_ALPHA = COL_BETA + 32
NCORES = 8
TOKQ = SEQ // 4
DBG = {}


class Buf:
    def __init__(self, h, name):
        self.h = h
        self.name = name
        self.lw = None
        self.rd = []

    def __getitem__(self, idx):
        return self.h[idx]


import types


def _freeze(fn):
    if fn is None or fn.__closure__ is None:
        return fn
    cells = []
    for c in fn.__closure__:
        try:
            cells.append(types.CellType(c.cell_contents))
        except ValueError:
            cells.append(c)
    return types.FunctionType(fn.__code__, fn.__globals__, fn.__name__, fn.__defaults__, tuple(cells))


class Op:
    __slots__ = ("eng", "fn", "deps", "is_dma", "need_inc", "sem", "val", "idx")


class Prog:
    ENGS = ("sync", "pe", "act", "dve", "pool")

    ARENA = 48640

    def __init__(self, nc):
        self.nc = nc
        self.ops = []
        self.last = {}
        self.dcnt = {}
        self.dlast = {}
        self.arena = None
        self.aptr = 0

    def perm(self, name, shape, dt=F32):
        return Buf(self.nc.alloc_sbuf_tensor(name, list(shape), dt), name)

    def sb(self, name, shape, dt=F32):
        if self.arena is None:
            self.arena = self.nc.alloc_sbuf_tensor("arena", [128, self.ARENA], F32)
        nel = 1
        for v in shape[1:]:
            nel *= v
        is4 = dt in (F32, mybir.dt.uint32, mybir.dt.int32)
        nfl = nel if is4 else (nel + 1) // 2
        nfl = (nfl + 7) // 8 * 8
        a = self.aptr
        self.aptr += nfl
        assert self.aptr <= self.ARENA, ("arena overflow", name, self.aptr)
        ap = self.arena[0:shape[0], a:a + nfl]
        if dt != F32:
            ap = ap.bitcast(dt)
        ap = ap[:, 0:nel]
        if len(shape) == 3:
            ap = ap.rearrange("p (a b) -> p a b", b=shape[2])
        return Buf(ap, name)

    def barrier(self):
        deps = set(self.last.values()) | set(self.dlast.values())
        for e in self.ENGS:
            o = self.op(e, None)
            o.deps = sorted(deps)
        self.aptr = 0

    def ps(self, name, shape, dt=F32):
        return Buf(self.nc.alloc_psum_tensor(name, list(shape), dt), name)

    def dram(self, name, shape, dt=F32, kind="Internal"):
        return Buf(self.nc.dram_tensor(name, list(shape), dt, kind=kind).ap(), name)

    def op(self, eng, fn, reads=(), writes=(), is_dma=False):
        o = Op()
        o.eng = eng
        o.fn = _freeze(fn)
        o.is_dma = is_dma
        o.need_inc = is_dma
        o.sem = None
        o.val = 0
        o.idx = len(self.ops)
        deps = set()
        for b in reads:
            if b.lw is not None:
                deps.add(b.lw)
        for b in writes:
            if b.lw is not None:
                deps.add(b.lw)
            deps.update(b.rd)
        ops = self.ops
        o.deps = sorted(d for d in deps
                        if not (eng == "pe" and ops[d].eng == "pe" and not ops[d].is_dma))
        ops.append(o)
        if fn is not None:
            self.last[eng] = o.idx
        if is_dma:
            k = self.dcnt.get(eng, 0)
            self.dcnt[eng] = k + 1
            prev = self.dlast.get((eng, k % 8))
            if prev is not None:
                o.deps = sorted(set(o.deps) | {prev})
            self.dlast[(eng, k % 8)] = o.idx
            o.val = (k % 8, 16 * (k // 8 + 1))
        for b in writes:
            b.lw = o.idx
            b.rd = []
        for b in reads:
            if b.lw != o.idx:
                b.rd.append(o.idx)
        return o

    def dma(self, out_ap, in_ap, reads=(), writes=(), q="sync", **kw):
        return self.op(q, lambda e: e.dma_start(out=out_ap, in_=in_ap, **kw), reads, writes, is_dma=True)

    def emit(self, final_wait_bufs=()):
        nc = self.nc
        ops = self.ops
        self.op("sync", None, reads=list(final_wait_bufs))
        for o in ops:
            for d in o.deps:
                ops[d].need_inc = True
        NDMA = 8
        with contextlib.ExitStack() as st:
            esem = {e: st.enter_context(nc.semaphore("s_" + e)) for e in self.ENGS}
            dsem = {e: [st.enter_context(nc.semaphore("d_%s%d" % (e, i))) for i in range(NDMA)]
                    for e in ("sync", "act", "pool")}
            ecnt = {e: 0 for e in self.ENGS}
            for o in ops:
                if o.is_dma:
                    k, v = o.val
                    o.sem = dsem[o.eng][k]
                    o.val = v
                elif o.need_inc:
                    ecnt[o.eng] += 1
                    o.sem = esem[o.eng]
                    o.val = ecnt[o.eng]
            per = {e: [o for o in ops if o.eng == e] for e in self.ENGS}

            def run(engname, eng):
                seen = {}
                for o in per[engname]:
                    for d in o.deps:
                        p = ops[d]
                        key = id(p.sem)
                        if seen.get(key, 0) >= p.val:
                            continue
                        eng.wait_ge(p.sem, p.val)
                        seen[key] = p.val
                    if o.fn is None:
                        continue
                    ins = o.fn(eng)
                    if o.sem is not None:
                        ins.then_inc(o.sem, 16 if o.is_dma else 1)

            with nc.Block() as block:
                @block.sync
                def _(e):
                    run("sync", e)

                @block.tensor
                def _(e):
                    run("pe", e)

                @block.scalar
                def _(e):
                    run("act", e)

                @block.vector
                def _(e):
                    run("dve", e)

                @block.gpsimd
                def _(e):
                    run("pool", e)


def make_consts():
    i = np.arange(128)
    same = (i[:, None] // 64) == (i[None, :] // 64)
    r = i[:, None]
    c = i[None, :]
    C = {}
    C["ident"] = np.eye(128)
    C["ones"] = np.ones((128, 128))
    NEG = -30000.0
    C["maskT0"] = np.where(same & (c >= r), 0.0, NEG)
    C["maskT1"] = np.where(same & (c <= r), 0.0, NEG)
    C["pos0"] = np.where(same & (r > c), 0.0, -NEG)
    C["pos1"] = np.where(same & (r < c), 0.0, -NEG)
    C["ucum0"] = (same & (r <= c)).astype(np.float64)
    C["ucum1"] = (same & (r >= c)).astype(np.float64)
    C["bsame"] = same.astype(np.float64)
    C["selA"] = np.broadcast_to((r < 64), (128, 128)).astype(np.float64)
    C["selB"] = np.broadcast_to((r >= 64), (128, 128)).astype(np.float64)
    names = list(C.keys())
    arr = np.concatenate([C[n] for n in names], axis=1).astype(np.float32)
    return names, arr


CONST_NAMES, CONST_ARR = make_consts()


def build(stop_after=99, dbg=False):
    nc = bass.Bass("TRN2", target_bir_lowering=False)
    P = Prog(nc)
    EI = "ExternalInput"
    x_d = P.dram("x", [SEQ, D], F32, EI)
    ctx_d = P.dram("ctx", [CTX, D], F32, EI)
    cin_d = P.dram("cin", [64, 128], F32, EI)
    wada_d = P.dram("w_ada", [D, 6 * D], F32, EI)
    bada_d = P.dram("b_ada", [1, 6 * D], F32, EI)
    gv_d = P.dram("gvecs", [128, 64], F32, EI)
    grow_d = P.dram("grows", [2, D], F32, EI)
    win_d = P.dram("w_in", [D, NIN], F32, EI)
    lcw_d = P.dram("lru_cw", [128, 16 * 5], F32, EI)
    lwa_d = P.dram("lru_w_a", [2 * 16 * 128, 128], F32, EI)
    lwx_d = P.dram("lru_w_x", [2 * 16 * 128, 128], F32, EI)
    lv_d = P.dram("lru_vecs", [128, 2 * 16 * 3], F32, EI)
    lng_d = P.dram("lru_ng", [128, 16], F32, EI)
    gcw_d = P.dram("gdn_cw", [128, 48 * 4], F32, EI)
    gsc_d = P.dram("gdn_sc", [1, 64], F32, EI)
    gng_d = P.dram("gdn_ng", [128, 1], F32, EI)
    wout_d = P.dram("w_out", [D, D], F32, EI)
    wg_d = P.dram("w_g", [D, DFF], F32, EI)
    wu_d = P.dram("w_u", [D, DFF], F32, EI)
    wd_d = P.dram("w_d", [DFF, D], F32, EI)
    cst_d = P.dram("consts", [128, CONST_ARR.shape[1]], F32, EI)
    xq_d = P.dram("x_q", [TOKQ, D], F32, EI)
    yidx_d = P.dram("yidx", [128, 256], mybir.dt.uint32, EI)
    out_d = P.dram("out", [TOKQ, D], F32, "ExternalOutput")
    IK = "ExternalOutput" if dbg else "Internal"
    mod_d = P.dram("mod_s", [2, 6 * D], F32, IK)
    p_parts = [(0, 4096), (4096, 10240), (10240, NIN)]
    p_bufs = [P.dram("p_s%d" % i, [b - a, NT], F32, IK if i == 2 else "Internal") for i, (a, b) in enumerate(p_parts)]

    class PD:
        def rows(self, r0, r1):
            for (a, b), buf in zip(p_parts, p_bufs):
                if a <= r0 and r1 <= b:
                    return buf, buf[r0 - a:r1 - a, :]
            raise AssertionError((r0, r1))
    pd = PD()
    y_d = P.dram("y_s", [32 * D, 256], F32, IK)
    y_v = y_d[:].rearrange("(tt c) t -> c tt t", c=D)
    qT_d = P.dram("qT_s", [128, NT], F32)
    kT_d = P.dram("kT_s", [128, NT], F32)
    kt_d = P.dram("kt_s", [NT, 128], F32)
    vt_d = P.dram("vt_s", [NT, 128], F32)

    cst = P.perm("cst", [128, CONST_ARR.shape[1]])
    P.dma(cst[:], cst_d[:], writes=[cst])

    def CS(name):
        k = CONST_NAMES.index(name)
        return cst[:, k * 128:(k + 1) * 128]
    identb = P.perm("identb", [128, 128], BF16)
    P.op("dve", lambda e: e.tensor_copy(identb[:], CS("ident")), [cst], [identb])

    PSB = [P.ps("psb%d" % i, [128, 512]) for i in range(8)]

    cc = P.sb("cc", [64, 128])
    sT = P.sb("sT", [128, 64])
    P.dma(cc[:], cin_d[:], writes=[cc])
    P.op("act", lambda e: e.activation(cc[:], cc[:], AF.Silu), [cc], [cc])
    P.op("pe", lambda e: e.transpose(PSB[0][:, 0:64], cc[:], CS("ident")[0:64, 0:64]), [cc, cst], [PSB[0]])
    P.op("act", lambda e: e.activation(sT[:], PSB[0][:, 0:64], AF.Copy), [PSB[0]], [sT])
    wa = [P.sb("wa%d" % i, [128, 3072]) for i in range(2)]
    mrow = P.sb("mrow", [2, 3072])
    brow = P.sb("brow", [2, 3072])
    it = 0
    for g in range(8):
        c0 = g * 3072
        P.dma(brow[0:1, :], bada_d[0:1, c0:c0 + 3072], writes=[brow])
        P.dma(brow[1:2, :], bada_d[0:1, c0:c0 + 3072], writes=[brow])
        for k in range(32):
            w = wa[it % 2]
            it += 1
            P.dma(w[:], wada_d[k * 128:(k + 1) * 128, c0:c0 + 3072], writes=[w])
            for n in range(6):
                P.op("pe", lambda e, w=w, n=n, k=k: e.matmul(PSB[n][0:2, :], sT[:, k:64:32], w[:, n * 512:(n + 1) * 512],
                                                          start=(k == 0), stop=(k == 31)), [sT, w], [PSB[n]])
        for n in range(6):
            P.op("dve", lambda e, n=n: e.tensor_tensor(mrow[:, n * 512:(n + 1) * 512], PSB[n][0:2, :], brow[:, n * 512:(n + 1) * 512], ALU.add),
                 [PSB[n], brow], [mrow])
        P.dma(mod_d[:, c0:c0 + 3072], mrow[:], reads=[mrow], writes=[mod_d])
    modT = P.perm("modT", [128, 2 * 192])
    mld = P.sb("mld", [96, 128])
    for j in range(2):
        for h in range(2):
            src = mod_d[j:j + 1, h * 96 * 128:(h + 1) * 96 * 128].rearrange("o (q p) -> (o q) p", p=128)
            P.dma(mld[:], src, reads=[mod_d], writes=[mld])
            P.op("pe", lambda e: e.transpose(PSB[0][:, 0:96], mld[:], CS("ident")[0:96, 0:96]), [mld, cst], [PSB[0]])
            P.op("act", lambda e, j=j, h=h: e.activation(modT[:, j * 192 + h * 96: j * 192 + (h + 1) * 96], PSB[0][:, 0:96], AF.Copy),
                 [PSB[0]], [modT])
    gv = P.perm("gv", [128, 64])
    P.dma(gv[:], gv_d[:], writes=[gv])
    GS = P.perm("GS", [128, 4 * 32])
    for idx, (j, which) in enumerate(((0, 0), (1, 0), (0, 1))):
        sc0 = j * 192 + (1 + 3 * which) * 32
        P.op("dve", lambda e, idx=idx, sc0=sc0, which=which: e.scalar_tensor_tensor(
            GS[:, idx * 32:(idx + 1) * 32], modT[:, sc0:sc0 + 32], 1.0, gv[:, which * 32:(which + 1) * 32], ALU.add, ALU.mult),
            [modT, gv], [GS])

    def SH(j, which, k):
        c = j * 192 + (3 * which) * 32 + k
        return modT[:, c:c + 1]

    def GSc(j, which, k):
        idx = {(0, 0): 0, (1, 0): 1, (0, 1): 2}[(j, which)]
        return GS[:, idx * 32 + k: idx * 32 + k + 1]

    st1 = P.perm("st1", [128, 8])
    st2 = P.perm("st2", [128, 8])
    lng = P.perm("lng", [128, 16])
    gng = P.perm("gng", [128, 1])
    PR = {}

    def rstd_from_ss(ss_ap, out_ap, n, bufs_r, bufs_w):
        P.op("dve", lambda e: e.tensor_scalar(out_ap, ss_ap, 1.0 / n, EPS, ALU.mult, ALU.add), bufs_r, bufs_w)
        P.op("act", lambda e: e.activation(out_ap, out_ap, AF.Sqrt), bufs_w, bufs_w)
        P.op("dve", lambda e: e.reciprocal(out_ap, out_ap), bufs_w, bufs_w)

    def norm_rows(src_buf, src_ap, tb):
        xn = PR["xn"]
        P.op("act", lambda e: e.activation(xn[:, tb, :], src_ap, AF.Square, accum_out=st1[:, tb:tb + 1]), [src_buf], [xn, st1])
        rstd_from_ss(st1[:, tb:tb + 1], st1[:, 4 + tb:5 + tb], D, [st1], [st1])
        P.op("dve", lambda e: e.tensor_scalar(xn[:, tb, :], src_ap, st1[:, 4 + tb:5 + tb], None, ALU.mult), [src_buf, st1], [xn])

    def make_hT(ntb, j, which):
        xn, hT = PR["xn"], PR["hT"]
        for k in range(32):
            pb = PSB[k % 2]
            pbv = pb[:].bitcast(BF16)
            for tb in range(ntb):
                P.op("pe", lambda e, k=k, tb=tb, pbv=pbv: e.transpose(pbv[:, tb * 128:(tb + 1) * 128], xn[:, tb, k * 128:(k + 1) * 128], identb[:]),
                     [xn, identb], [pb])
            P.op("act", lambda e, k=k, pbv=pbv: e.activation(hT[k][:, 0:ntb * 128], pbv[:, 0:ntb * 128], AF.Identity,
                                                          scale=GSc(j, which, k), bias=SH(j, which, k)), [pb, GS, modT], [hT[k]])

    P.barrier()
    xrow = [P.sb("xrow%d" % i, [128, D]) for i in range(2)]
    PR["xn"] = P.sb("xn", [128, 4, D], BF16)
    hT = [P.sb("hT%d" % k, [128, 512], BF16) for k in range(32)]
    PR["hT"] = hT
    wst = [P.sb("wst%d" % i, [128, 32, 128]) for i in range(2)]
    wbf = [P.sb("wbf%d" % i, [128, 32, 128], BF16) for i in range(2)]
    pst = [P.sb("pst%d" % i, [128, 512]) for i in range(2)]
    NCH = (NIN + 127) // 128
    tiles = [(t * 512, 4, 0) for t in range(16)] + [(SEQ, 2, 1)]
    if DBG.get('skip1'):
        tiles = []
    wi = 0
    for (t0, ntb, j) in tiles:
        for tb in range(ntb):
            xr = xrow[tb % 2]
            src = x_d[t0 + tb * 128: t0 + (tb + 1) * 128, :] if j == 0 else ctx_d[tb * 128:(tb + 1) * 128, :]
            P.dma(xr[:], src, writes=[xr])
            norm_rows(xr, xr[:], tb)
        make_hT(ntb, j, 0)
        T = ntb * 128
        for c in range(NCH):
            m = min(128, NIN - c * 128)
            ws, wb = wst[wi % 2], wbf[wi % 2]
            wi += 1
            P.dma(ws[:, :, 0:m], win_d[:, c * 128:c * 128 + m].rearrange("(k p) n -> p k n", p=128), writes=[ws])
            P.op("pool", lambda e, ws=ws, wb=wb, m=m: e.tensor_copy(wb[:, :, 0:m], ws[:, :, 0:m]), [ws], [wb])
            pb = PSB[2 + c % 4]
            for k in range(32):
                P.op("pe", lambda e, k=k, wb=wb, pb=pb, m=m, T=T: e.matmul(pb[0:m, 0:T], wb[:, k, 0:m], hT[k][:, 0:T], start=(k == 0), stop=(k == 31)),
                     [wb, hT[k]], [pb])
            po = pst[c % 2]
            eng = "act" if c % 2 == 0 else "dve"
            if eng == "act":
                P.op("act", lambda e, po=po, pb=pb, m=m, T=T: e.activation(po[0:m, 0:T], pb[0:m, 0:T], AF.Copy), [pb], [po])
            else:
                P.op("dve", lambda e, po=po, pb=pb, m=m, T=T: e.tensor_copy(po[0:m, 0:T], pb[0:m, 0:T]), [pb], [po])
            pbuf, pap = pd.rows(c * 128, c * 128 + m)
            P.dma(pap[:, t0:t0 + T], po[0:m, 0:T], reads=[po], writes=[pbuf])

    if stop_after < 2:
        P.emit(final_wait_bufs=[mod_d, p_bufs[2]])
        return nc
    P.barrier()
    W = NT
    big = [P.sb("big%d" % i, [128, W]) for i in range(4)]
    lcw = P.sb("lcw", [128, 80])
    lv = P.sb("lv", [128, 96])
    P.dma(lcw[:], lcw_d[:], writes=[lcw])
    P.dma(lv[:], lv_d[:], writes=[lv])
    P.dma(lng[:], lng_d[:], writes=[lng])
    lcs = P.sb("lcs", [128, 64])
    lam_v = lv[:, 2:96:3]
    P.op("act", lambda e: e.activation(lcs[:, 0:32], lam_v, AF.Exp, scale=-1.0), [lv], [lcs])
    P.op("dve", lambda e: e.tensor_scalar(lcs[:, 0:32], lcs[:, 0:32], 1.0, None, ALU.add), [lcs], [lcs])
    P.op("act", lambda e: e.activation(lcs[:, 0:32], lcs[:, 0:32], AF.Ln), [lcs], [lcs])
    P.op("dve", lambda e: e.tensor_scalar(lcs[:, 32:64], lcs[:, 0:32], -16.0, None, ALU.mult), [lcs], [lcs])
    P.op("dve", lambda e: e.tensor_scalar(lcs[:, 0:32], lcs[:, 0:32], -8.0, None, ALU.mult), [lcs], [lcs])
    lw = [P.sb("lw%d" % i, [128, 128]) for i in range(4)]
    tmpc = [P.sb("tmpc%d" % i, [128, 512]) for i in range(6)]

    def conv_seg(dst, src, a, b, wt, wcol, bias_ap):
        if bias_ap is not None:
            P.op("dve", lambda e: e.tensor_scalar(dst[:, a:b], src[:, a:b], wt[:, wcol + 2:wcol + 3], bias_ap, ALU.mult, ALU.add), [src, wt], [dst])
        else:
            P.op("dve", lambda e: e.tensor_scalar(dst[:, a:b], src[:, a:b], wt[:, wcol + 2:wcol + 3], None, ALU.mult), [src, wt], [dst])
        for (jj, off) in ((0, -2), (1, -1), (3, 1)):
            if off < 0:
                da, db, sa, sb_ = a - off, b, a, b + off
            else:
                da, db, sa, sb_ = a, b - off, a + off, b
            P.op("dve", lambda e, jj=jj, da=da, db=db, sa=sa, sb_=sb_: e.scalar_tensor_tensor(
                dst[:, da:db], src[:, sa:sb_], wt[:, wcol + jj:wcol + jj + 1], dst[:, da:db], ALU.mult, ALU.add), [src, wt, dst], [dst])

    def gelu_inplace(buf, a, b, tmp):
        P.op("pool", lambda e: e.tensor_tensor(tmp[:, a:b], buf[:, a:b], buf[:, a:b], ALU.mult), [buf], [tmp])
        P.op("pool", lambda e: e.tensor_scalar(tmp[:, a:b], tmp[:, a:b], 0.044715, 1.0, ALU.mult, ALU.add), [tmp], [tmp])
        P.op("pool", lambda e: e.tensor_tensor(tmp[:, a:b], tmp[:, a:b], buf[:, a:b], ALU.mult), [tmp, buf], [tmp])
        P.op("act", lambda e: e.activation(tmp[:, a:b], tmp[:, a:b], AF.Sigmoid, scale=1.5957691216), [tmp], [tmp])
        P.op("pool", lambda e: e.tensor_tensor(buf[:, a:b], buf[:, a:b], tmp[:, a:b], ALU.mult), [buf, tmp], [buf])

    for n in range(0 if DBG.get('skip2') else 16):
        u, gt, xl, hf = big
        pbuf, pap = pd.rows(n * 128, (n + 1) * 128)
        P.dma(u[:], pap, reads=[pbuf], writes=[u])
        pbuf, pap = pd.rows(COL_G + n * 128, COL_G + (n + 1) * 128)
        P.dma(gt[:, 0:SEQ], pap[:, 0:SEQ], reads=[pbuf], writes=[gt])
        conv_seg(xl, u, 0, SEQ, lcw, n * 5, lcw[:, n * 5 + 4:n * 5 + 5])
        conv_seg(xl, u, SEQ, NT, lcw, n * 5, lcw[:, n * 5 + 4:n * 5 + 5])
        gelu_inplace(gt, 0, SEQ, u)
        for d in range(2):
            dn = d * 16 + n
            P.dma(lw[2 * d][:], lwa_d[dn * 128:(dn + 1) * 128, :], writes=[lw[2 * d]])
            P.dma(lw[2 * d + 1][:], lwx_d[dn * 128:(dn + 1) * 128, :], writes=[lw[2 * d + 1]])
            chunks = [(SEQ, NT)] + [(c * 512, (c + 1) * 512) for c in range(16)]
            if d == 1:
                chunks = [(SEQ, NT)] + [(c * 512, (c + 1) * 512) for c in range(15, -1, -1)]
            prev = None
            for ci, (a, b) in enumerate(chunks):
                L = b - a
                pa, px = PSB[(2 * ci) % 8], PSB[(2 * ci + 1) % 8]
                P.op("pe", lambda e, pa=pa, a=a, b=b, L=L: e.matmul(pa[:, 0:L], lw[2 * d][:], xl[:, a:b], start=True, stop=True), [lw[2 * d], xl], [pa])
                P.op("pe", lambda e, px=px, a=a, b=b, L=L: e.matmul(px[:, 0:L], lw[2 * d + 1][:], xl[:, a:b], start=True, stop=True), [lw[2 * d + 1], xl], [px])
                r_, i_, a_, a2_, b_, hb_ = tmpc
                P.op("act", lambda e, pa=pa, L=L: e.activation(r_[:, 0:L], pa[:, 0:L], AF.Sigmoid, bias=lv[:, dn * 3:dn * 3 + 1]), [pa, lv], [r_])
                P.op("act", lambda e, px=px, L=L: e.activation(i_[:, 0:L], px[:, 0:L], AF.Sigmoid, bias=lv[:, dn * 3 + 1:dn * 3 + 2]), [px, lv], [i_])
                P.op("act", lambda e, L=L: e.activation(a_[:, 0:L], r_[:, 0:L], AF.Exp, scale=lcs[:, dn:dn + 1]), [r_, lcs], [a_])
                P.op("act", lambda e, L=L: e.activation(a2_[:, 0:L], r_[:, 0:L], AF.Exp, scale=lcs[:, 32 + dn:33 + dn]), [r_, lcs], [a2_])
                P.op("dve", lambda e, L=L: e.tensor_scalar(a2_[:, 0:L], a2_[:, 0:L], -1.0, 1.0, ALU.mult, ALU.add), [a2_], [a2_])
                P.op("dve", lambda e, L=L: e.tensor_scalar(a2_[:, 0:L], a2_[:, 0:L], 1e-30, None, ALU.max), [a2_], [a2_])
                P.op("act", lambda e, L=L: e.activation(a2_[:, 0:L], a2_[:, 0:L], AF.Sqrt), [a2_], [a2_])
                P.op("pool", lambda e, a=a, b=b, L=L: e.tensor_tensor(i_[:, 0:L], i_[:, 0:L], xl[:, a:b], ALU.mult), [i_, xl], [i_])
                P.op("pool", lambda e, L=L: e.tensor_tensor(b_[:, 0:L], i_[:, 0:L], a2_[:, 0:L], ALU.mult), [i_, a2_], [b_])
                if d == 0:
                    init = 0.0 if prev is None else hf[:, prev:prev + 1]
                    P.op("dve", lambda e, a=a, b=b, L=L, init=init: e.tensor_tensor_scan(hf[:, a:b], a_[:, 0:L], b_[:, 0:L], init, ALU.mult, ALU.add),
                         [a_, b_, hf], [hf])
                    prev = b - 1
                else:
                    init = 0.0 if prev is None else u[:, prev:prev + 1]
                    P.op("dve", lambda e, a=a, b=b, L=L, init=init: e.tensor_tensor_scan(u[:, a:b][:, ::-1], a_[:, 0:L][:, ::-1], b_[:, 0:L][:, ::-1], init, ALU.mult, ALU.add),
                         [a_, b_, u], [u])
                    prev = a
                    if a < SEQ:
                        P.op("pool", lambda e, a=a, b=b, L=L: e.tensor_tensor(hb_[:, 0:L], u[:, a:b], hf[:, a:b], ALU.add), [u, hf], [hb_])
                        P.op("pool", lambda e, a=a, b=b, L=L: e.tensor_tensor(hb_[:, 0:L], hb_[:, 0:L], gt[:, a:b], ALU.mult), [hb_, gt], [hb_])
                        P.dma(y_v[n * 128:(n + 1) * 128, a // 256:b // 256, :], hb_[:, 0:L].rearrange("p (a t) -> p a t", t=256), reads=[hb_], writes=[y_d])

    if stop_after < 3:
        P.emit(final_wait_bufs=[mod_d, p_bufs[2], y_d])
        return nc
    P.barrier()
    big = [P.sb("big%d" % i, [128, W]) for i in range(4)]
    gcw = P.sb("gcw", [128, 192])
    P.dma(gcw[:], gcw_d[:], writes=[gcw])
    gsc = P.sb("gsc", [128, 64])
    P.dma(gsc[:], gsc_d[0:1, :].partition_broadcast(128), writes=[gsc])
    P.op("act", lambda e: e.activation(gsc[:, 0:32], gsc[:, 0:32], AF.Exp), [gsc], [gsc])
    P.op("dve", lambda e: e.tensor_scalar(gsc[:, 0:32], gsc[:, 0:32], -1.0, None, ALU.mult), [gsc], [gsc])
    P.dma(gng[:], gng_d[:], writes=[gng])
    NP_ = 66
    Gb = P.sb("Gb", [128, 2 * NP_])
    Ga = P.sb("Ga", [128, 2 * NP_])
    gcum = P.sb("gcum", [128, 2 * NP_])
    eg = P.sb("eg", [128, 2 * NP_])
    etail = P.sb("etail", [128, 2 * NP_])
    glA = P.sb("glA", [128, 2 * NP_])
    glB = P.sb("glB", [128, 2 * NP_])
    nbeta = P.sb("nbeta", [128, 2 * NP_])
    beg = P.sb("beg", [128, 2 * NP_])
    S = P.sb("S", [128, 128])
    oacc = P.sb("oacc", [128, 64, 128])
    pt = {nm: [P.sb("pt_%s%d" % (nm, i), [128, 128]) for i in range(2)] for nm in ("kT", "qT", "kt", "vt")}
    gt_ = {nm: P.sb("g_" + nm, [128, 128]) for nm in ("diag", "E1", "dT", "E2", "dS", "N", "Nt", "N2", "Nt2", "Tt", "Tt2", "vb", "kbeg",
                                                       "u", "wT", "ktl", "inT", "vnew", "iv", "op", "osum")}

    def prep(hh, which, dst_T, dst_t):
        raw, cm, cv, sq = big
        row0 = COL_QKV + which * 2048 + hh * 128
        pbuf, pap = pd.rows(row0, row0 + 128)
        P.dma(raw[:], pap, reads=[pbuf], writes=[raw])
        P.op("pool", lambda e: e.tensor_copy(cm[:, 0:SEQ].rearrange("p (c r) -> p c r", r=128), raw[:, 0:SEQ].rearrange("p (r c) -> p c r", c=64)), [raw], [cm])
        P.op("pool", lambda e: e.tensor_copy(cm[:, SEQ:NT], raw[:, SEQ:NT]), [raw], [cm])
        wcol = (which * 16 + hh) * 4
        conv_seg(cv, cm, 0, SEQ, gcw, wcol, None)
        conv_seg(cv, cm, SEQ, NT, gcw, wcol, None)
        P.op("act", lambda e: e.activation(cv[:], cv[:], AF.Silu), [cv], [cv])
        if which < 2:
            P.op("act", lambda e: e.activation(sq[:], cv[:], AF.Square), [cv], [sq])
            for c in range(0, NT, 512):
                L = min(512, NT - c)
                pb = PSB[(c // 512) % 4]
                P.op("pe", lambda e, c=c, L=L, pb=pb: e.matmul(pb[:, 0:L], CS("ones"), sq[:, c:c + L], start=True, stop=True), [cst, sq], [pb])
                P.op("dve", lambda e, c=c, L=L, pb=pb: e.tensor_scalar(sq[:, c:c + L], pb[:, 0:L], EPS, None, ALU.add), [pb], [sq])
            P.op("act", lambda e: e.activation(sq[:], sq[:], AF.Sqrt), [sq], [sq])
            P.op("dve", lambda e: e.reciprocal(sq[:], sq[:]), [sq], [sq])
            scl = (128.0 ** -0.5) if which == 0 else 1.0
            P.op("dve", lambda e: e.scalar_tensor_tensor(cv[:], cv[:], scl, sq[:], ALU.mult, ALU.mult), [cv, sq], [cv])
        if dst_T is not None:
            P.dma(dst_T[:], cv[:], reads=[cv], writes=[dst_T])
        if dst_t is not None:
            for m in range(NP_):
                pb = PSB[4 + m % 4]
                P.op("pe", lambda e, m=m, pb=pb: e.transpose(pb[:, 0:128], cv[:, m * 128:(m + 1) * 128], CS("ident")), [cv, cst], [pb])
                if m % 2 == 0:
                    P.op("act", lambda e, m=m, pb=pb: e.activation(raw[:, (m % 8) * 128:(m % 8 + 1) * 128], pb[:, 0:128], AF.Copy), [pb], [raw])
                else:
                    P.op("dve", lambda e, m=m, pb=pb: e.tensor_copy(raw[:, (m % 8) * 128:(m % 8 + 1) * 128], pb[:, 0:128]), [pb], [raw])
                if m % 8 == 7 or m == NP_ - 1:
                    m0 = m - (m % 8)
                    cnt = m - m0 + 1
                    P.dma(dst_t[m0 * 128:(m + 1) * 128, :].rearrange("(a p) c -> p a c", p=128),
                          raw[:, 0:cnt * 128].rearrange("p (a c) -> p a c", c=128), reads=[raw], writes=[dst_t])

    def mm(out_buf, out_ap, l_ap, r_ap, rd):
        P.op("pe", lambda e: e.matmul(out_ap, l_ap, r_ap, start=True, stop=True), rd, [out_buf])

    ev_i = [0]

    def evac(dst_buf, dst_ap, src_buf, src_ap):
        ev_i[0] += 1
        if ev_i[0] % 2:
            P.op("act", lambda e: e.activation(dst_ap, src_ap, AF.Copy), [src_buf], [dst_buf])
        else:
            P.op("dve", lambda e: e.tensor_copy(dst_ap, src_ap), [src_buf], [dst_buf])

    for hh in range(DBG.get('nh', 16)):
        prep(hh, 0, qT_d, None)
        prep(hh, 1, kT_d, kt_d)
        prep(hh, 2, None, vt_d)
        for d in range(2):
            for (src_t, col) in ((Gb, COL_BETA), (Ga, COL_ALPHA)):
                rowi = col + d * 16 + hh
                pbuf, pap = pd.rows(rowi, rowi + 1)
                P.dma(src_t[:, d * NP_: d * NP_ + 64], pap[:, 0:SEQ].rearrange("o (i m) -> (o i) m", m=64), reads=[pbuf], writes=[src_t])
                P.dma(src_t[:, d * NP_ + 64: d * NP_ + 66], pap[:, SEQ:NT].rearrange("o (m i) -> (o i) m", i=128),
                      reads=[pbuf], writes=[src_t], allow_slow_non_contiguous=True)
        P.op("act", lambda e: e.activation(Gb[:], Gb[:], AF.Sigmoid), [Gb], [Gb])
        for d in range(2):
            sl = slice(d * NP_, (d + 1) * NP_)
            P.op("act", lambda e, sl=sl, d=d: e.activation(Ga[:, sl], Ga[:, sl], AF.Exp, bias=gsc[:, 32 + d * 16 + hh: 33 + d * 16 + hh]), [Ga, gsc], [Ga])
            P.op("dve", lambda e, sl=sl: e.tensor_scalar(Ga[:, sl], Ga[:, sl], 1.0, None, ALU.add), [Ga], [Ga])
            P.op("act", lambda e, sl=sl: e.activation(Ga[:, sl], Ga[:, sl], AF.Ln), [Ga], [Ga])
            P.op("dve", lambda e, sl=sl, d=d: e.tensor_scalar(Ga[:, sl], Ga[:, sl], gsc[:, d * 16 + hh: d * 16 + hh + 1], None, ALU.mult), [Ga, gsc], [Ga])
            pb = PSB[0]
            mm(pb, pb[:, 0:NP_], CS("ucum%d" % d), Ga[:, sl], [cst, Ga])
            evac(gcum, gcum[:, sl], pb, pb[:, 0:NP_])
            P.op("act", lambda e, sl=sl: e.activation(eg[:, sl], gcum[:, sl], AF.Exp), [gcum], [eg])
            pb = PSB[1]
            mm(pb, pb[:, 0:NP_], CS("bsame"), Ga[:, sl], [cst, Ga])
            P.op("dve", lambda e, sl=sl, pb=pb: e.tensor_tensor(etail[:, sl], pb[:, 0:NP_], gcum[:, sl], ALU.subtract), [pb, gcum], [etail])
            P.op("act", lambda e, sl=sl: e.activation(etail[:, sl], etail[:, sl], AF.Exp), [etail], [etail])
            pb = PSB[2]
            mm(pb, pb[:, 0:NP_], CS("selA"), Ga[:, sl], [cst, Ga])
            P.op("act", lambda e, sl=sl, pb=pb: e.activation(glA[:, sl], pb[:, 0:NP_], AF.Exp), [pb], [glA])
            pb = PSB[3]
            mm(pb, pb[:, 0:NP_], CS("selB"), Ga[:, sl], [cst, Ga])
            P.op("act", lambda e, sl=sl, pb=pb: e.activation(glB[:, sl], pb[:, 0:NP_], AF.Exp), [pb], [glB])
        P.op("dve", lambda e: e.tensor_scalar(nbeta[:], Gb[:], -1.0, None, ALU.mult), [Gb], [nbeta])
        P.op("dve", lambda e: e.tensor_tensor(beg[:], Gb[:], eg[:], ALU.mult), [Gb, eg], [beg])

        li = 0
        for d in range(2):
            P.op("dve", lambda e: e.memset(S[:], 0.0), [], [S])
            order = [64, 65] + list(range(64)) if d == 0 else [65, 64] + list(range(63, -1, -1))
            order = order[:DBG.get('npair', 99)]
            for m in order:
                col = d * NP_ + m
                lat = m < 64
                tl = {nm: pt[nm][li % 2] for nm in pt}
                li += 1
                P.dma(tl["kT"][:], kT_d[:, m * 128:(m + 1) * 128], reads=[kT_d], writes=[tl["kT"]])
                P.dma(tl["kt"][:], kt_d[m * 128:(m + 1) * 128, :], reads=[kt_d], writes=[tl["kt"]])
                P.dma(tl["vt"][:], vt_d[m * 128:(m + 1) * 128, :], reads=[vt_d], writes=[tl["vt"]])
                if lat:
                    P.dma(tl["qT"][:], qT_d[:, m * 128:(m + 1) * 128], reads=[qT_d], writes=[tl["qT"]])
                G = gt_
                gc_col = gcum[:, col:col + 1]
                P.op("dve", lambda e, gc_col=gc_col: e.tensor_scalar(G["diag"][:], CS("ident"), gc_col, None, ALU.mult), [cst, gcum], [G["diag"]])
                pR = PSB[0]
                mm(pR, pR[:, 0:128], CS("ones"), G["diag"][:], [cst, G["diag"]])
                P.op("dve", lambda e, gc_col=gc_col, d=d: e.scalar_tensor_tensor(G["E2"][:], pR[:, 0:128], gc_col, CS("pos%d" % d), ALU.subtract, ALU.add),
                     [pR, gcum, cst], [G["E2"]])
                P.op("act", lambda e: e.activation(G["dS"][:], G["E2"][:], AF.Exp, scale=-1.0), [G["E2"]], [G["dS"]])
                if lat:
                    P.op("dve", lambda e, gc_col=gc_col, d=d: e.scalar_tensor_tensor(G["E1"][:], pR[:, 0:128], gc_col, CS("maskT%d" % d), ALU.subtract, ALU.add),
                         [pR, gcum, cst], [G["E1"]])
                    P.op("act", lambda e: e.activation(G["dT"][:], G["E1"][:], AF.Exp), [G["E1"]], [G["dT"]])
                cut = DBG.get('cut', 99)
                if cut < 1:
                    continue
                pK = PSB[1]
                mm(pK, pK[:, 0:128], tl["kT"][:], tl["kT"][:], [tl["kT"]])
                P.op("dve", lambda e, col=col: e.scalar_tensor_tensor(G["N"][:], pK[:, 0:128], nbeta[:, col:col + 1], G["dS"][:], ALU.mult, ALU.mult),
                     [pK, nbeta, G["dS"]], [G["N"]])
                if DBG.get('sub', 9) < 2:
                    continue
                pT = PSB[2]
                P.op("pe", lambda e: e.transpose(pT[:, 0:128], G["N"][:], CS("ident")), [G["N"], cst], [pT])
                evac(G["Nt"], G["Nt"][:], pT, pT[:, 0:128])
                if DBG.get('sub', 9) < 3:
                    continue
                P.op("dve", lambda e: e.tensor_tensor(G["Tt"][:], G["Nt"][:], CS("ident"), ALU.add), [G["Nt"], cst], [G["Tt"]])
                Ncur, Ntcur, Nnx, Ntnx, Tcur, Tnx = "N", "Nt", "N2", "Nt2", "Tt", "Tt2"
                if cut < 2:
                    continue
                for lev in range(1, 6):
                    p1 = PSB[3]
                    mm(p1, p1[:, 0:128], G[Ntcur][:], G[Ncur][:], [G[Ntcur], G[Ncur]])
                    evac(G[Nnx], G[Nnx][:], p1, p1[:, 0:128])
                    if lev < 5:
                        p2 = PSB[4]
                        mm(p2, p2[:, 0:128], G[Ncur][:], G[Ntcur][:], [G[Ntcur], G[Ncur]])
                        evac(G[Ntnx], G[Ntnx][:], p2, p2[:, 0:128])
                    p3 = PSB[5]
                    mm(p3, p3[:, 0:128], G[Nnx][:], G[Tcur][:], [G[Nnx], G[Tcur]])
                    P.op("dve", lambda e, Tnx=Tnx, Tcur=Tcur, p3=p3: e.tensor_tensor(G[Tnx][:], p3[:, 0:128], G[Tcur][:], ALU.add), [p3, G[Tcur]], [G[Tnx]])
                    Ncur, Nnx = Nnx, Ncur
                    Ntcur, Ntnx = Ntnx, Ntcur
                    Tcur, Tnx = Tnx, Tcur
                TT = G[Tcur]
                if cut < 3:
                    continue
                P.op("act", lambda e, col=col: e.activation(G["vb"][:], tl["vt"][:], AF.Copy, scale=Gb[:, col:col + 1]), [tl["vt"], Gb], [G["vb"]])
                P.op("act", lambda e, col=col: e.activation(G["kbeg"][:], tl["kt"][:], AF.Copy, scale=beg[:, col:col + 1]), [tl["kt"], beg], [G["kbeg"]])
                P.op("dve", lambda e, col=col: e.tensor_scalar(G["ktl"][:], tl["kt"][:], etail[:, col:col + 1], None, ALU.mult), [tl["kt"], etail], [G["ktl"]])
                pU = PSB[6]
                mm(pU, pU[:, 0:128], TT[:], G["vb"][:], [TT, G["vb"]])
                evac(G["u"], G["u"][:], pU, pU[:, 0:128])
                pW = PSB[7]
                mm(pW, pW[:, 0:128], G["kbeg"][:], TT[:], [TT, G["kbeg"]])
                evac(G["wT"], G["wT"][:], pW, pW[:, 0:128])
                if lat:
                    pQ = PSB[0]
                    mm(pQ, pQ[:, 0:128], tl["kT"][:], tl["qT"][:], [tl["kT"], tl["qT"]])
                    P.op("dve", lambda e: e.tensor_tensor(G["inT"][:], pQ[:, 0:128], G["dT"][:], ALU.mult), [pQ, G["dT"]], [G["inT"]])
                if cut < 4:
                    continue
                halves = [(0, 64, glA), (64, 128, glB)]
                if d == 1:
                    halves = halves[::-1]
                for (a, b, gl) in halves:
                    pws = PSB[1]
                    mm(pws, pws[:, 0:128], G["wT"][:], S[:], [G["wT"], S])
                    P.op("dve", lambda e, a=a, b=b, pws=pws: e.tensor_tensor(G["vnew"][a:b, :], G["u"][a:b, :], pws[a:b, 0:128], ALU.subtract),
                         [G["u"], pws], [G["vnew"]])
                    if lat:
                        pqs = PSB[2]
                        mm(pqs, pqs[:, 0:128], tl["qT"][:], S[:], [tl["qT"], S])
                        piv = PSB[3]
                        mm(piv, piv[:, 0:128], G["inT"][a:b, :], G["vnew"][a:b, :], [G["inT"], G["vnew"]])
                        P.op("act", lambda e, a=a, b=b, piv=piv: e.activation(G["iv"][a:b, :], piv[a:b, 0:128], AF.Copy), [piv], [G["iv"]])
                        P.op("dve", lambda e, a=a, b=b, pqs=pqs, col=col: e.scalar_tensor_tensor(G["op"][a:b, :], pqs[a:b, 0:128], eg[a:b, col:col + 1], G["iv"][a:b, :], ALU.mult, ALU.add),
                             [pqs, eg, G["iv"]], [G["op"]])
                    pds = PSB[4]
                    mm(pds, pds[:, 0:128], G["ktl"][a:b, :], G["vnew"][a:b, :], [G["ktl"], G["vnew"]])
                    P.op("dve", lambda e, gl=gl, col=col, pds=pds: e.scalar_tensor_tensor(S[:], S[:], gl[:, col:col + 1], pds[:, 0:128], ALU.mult, ALU.add),
                         [S, gl, pds], [S])
                if lat:
                    if d == 0:
                        P.op("pool", lambda e, m=m: e.tensor_copy(oacc[:, m, :], G["op"][:]), [G["op"]], [oacc])
                    else:
                        P.op("pool", lambda e, m=m: e.tensor_tensor(G["osum"][:], oacc[:, m, :], G["op"][:], ALU.add), [G["op"], oacc], [G["osum"]])
                        po = PSB[5]
                        P.op("pe", lambda e, po=po: e.transpose(po[:, 0:128], G["osum"][:], CS("ident")), [G["osum"], cst], [po])
                        P.op("act", lambda e, m=m, po=po: e.activation(big[0][:, m:SEQ:64], po[:, 0:128], AF.Copy), [po], [big[0]])
        oT, zT, sq2, _ = big
        pbuf, pap = pd.rows(COL_Z + hh * 128, COL_Z + (hh + 1) * 128)
        P.dma(zT[:, 0:SEQ], pap[:, 0:SEQ], reads=[pbuf], writes=[zT])
        P.op("act", lambda e: e.activation(zT[:, 0:SEQ], zT[:, 0:SEQ], AF.Silu), [zT], [zT])
        P.op("act", lambda e: e.activation(sq2[:, 0:SEQ], oT[:, 0:SEQ], AF.Square), [oT], [sq2])
        for c in range(0, SEQ, 512):
            pb = PSB[(c // 512) % 4]
            P.op("pe", lambda e, c=c, pb=pb: e.matmul(pb[:, 0:512], CS("ones"), sq2[:, c:c + 512], start=True, stop=True), [cst, sq2], [pb])
            P.op("dve", lambda e, c=c, pb=pb: e.tensor_scalar(sq2[:, c:c + 512], pb[:, 0:512], 1.0 / 128, EPS, ALU.mult, ALU.add), [pb], [sq2])
        P.op("act", lambda e: e.activation(sq2[:, 0:SEQ], sq2[:, 0:SEQ], AF.Sqrt), [sq2], [sq2])
        P.op("dve", lambda e: e.reciprocal(sq2[:, 0:SEQ], sq2[:, 0:SEQ]), [sq2], [sq2])
        P.op("dve", lambda e: e.scalar_tensor_tensor(oT[:, 0:SEQ], oT[:, 0:SEQ], gng[:, 0:1], sq2[:, 0:SEQ], ALU.mult, ALU.mult), [oT, gng, sq2], [oT])
        P.op("pool", lambda e: e.tensor_tensor(oT[:, 0:SEQ], oT[:, 0:SEQ], zT[:, 0:SEQ], ALU.mult), [oT, zT], [oT])
        P.dma(y_v[2048 + hh * 128: 2048 + (hh + 1) * 128, :, :], oT[:, 0:SEQ].rearrange("p (a t) -> p a t", t=256), reads=[oT], writes=[y_d])

    if stop_after < 4:
        P.emit(final_wait_bufs=[mod_d, p_bufs[2], y_d])
        return nc
    P.barrier()
    TB = 2
    T = TB * 128
    gprod_d = P.dram("gprod_s", [2, D], F32)
    R1 = P.sb("R1", [128, 2 * D])
    R2 = P.sb("R2", [128, 2 * D])
    gmul = P.sb("gmul", [128, D])
    P.dma(R1[:, 0:D], grow_d[0:1, :].partition_broadcast(128), writes=[R1])
    P.dma(R1[:, D:2 * D], grow_d[1:2, :].partition_broadcast(128), writes=[R1])
    P.dma(R2[:, 0:D], mod_d[0:1, 2 * D:3 * D].partition_broadcast(128), reads=[mod_d], writes=[R2])
    P.dma(R2[:, D:2 * D], mod_d[0:1, 5 * D:6 * D].partition_broadcast(128), reads=[mod_d], writes=[R2])
    P.op("pool", lambda e: e.tensor_tensor(R1[:], R1[:], R2[:], ALU.mult), [R1, R2], [R1])
    P.dma(gprod_d[0:1, :], R1[0:1, 0:D], reads=[R1], writes=[gprod_d])
    P.dma(gprod_d[1:2, :], R1[0:1, D:2 * D], reads=[R1], writes=[gprod_d])
    Mb = P.sb("Mb", [128, 32, T], BF16)
    PR["xn"] = Buf(Mb[:].rearrange("p a b -> p (a b)").rearrange("p (a b) -> p a b", b=D), "xnv")
    Mb.alias = PR["xn"]
    hT = [P.sb("hT%d" % k, [128, T], BF16) for k in range(32)]
    PR["hT"] = hT
    aT = [P.sb("aT%d" % j, [128, T], BF16) for j in range(86)]
    rbc = P.sb("rbc", [128, T])
    sqt = P.sb("sqt", [128, T])
    sil = P.sb("sil", [128, T])
    wsl = [P.sb("wsl%d" % i, [128, 2048]) for i in range(2)]
    wslb = [P.sb("wslb%d" % i, [128, 2048], BF16) for i in range(2)]
    cast_i = [0]

    def cast(dst_buf, dst_ap, src_buf, src_ap):
        cast_i[0] += 1
        eng = ("pool", "pool", "dve")[cast_i[0] % 3]
        P.op(eng, lambda e: e.tensor_copy(dst_ap, src_ap), [src_buf], [dst_buf])

    def tok_epilogue(RB, tb, which):
        osl = RB[:, tb * D:(tb + 1) * D]
        P.op("act", lambda e: e.activation(gmul[:], osl, AF.Square, accum_out=st2[:, tb:tb + 1]), [RB], [gmul, st2])
        rstd_from_ss(st2[:, tb:tb + 1], st2[:, 4 + tb:5 + tb], D, [st2], [st2])
        P.dma(gmul[:], gprod_d[which:which + 1, :].partition_broadcast(128), reads=[gprod_d], writes=[gmul])
        P.op("dve", lambda e: e.scalar_tensor_tensor(osl, osl, st2[:, 4 + tb:5 + tb], gmul[:], ALU.mult, ALU.mult), [RB, st2, gmul], [RB])
        return osl

    wj = 0
    MX = [Mb]
    yidx = P.sb("yidx", [128, 256], mybir.dt.uint32)
    P.dma(yidx[:], yidx_d[:], writes=[yidx])
    for ti in range(DBG.get('ntile', TOKQ // T)):
        t0 = ti * T
        ymix = R1
        for k in range(32):
            P.op("pool", lambda e, k=k, ti=ti: e.indirect_dma_start(
                out=ymix[:, k * T:(k + 1) * T], out_offset=None, in_=y_d[:],
                in_offset=bass.IndirectOffsetOnAxis(ap=yidx[:, ti * 32 + k: ti * 32 + k + 1], axis=0)),
                [y_d, yidx], [ymix], is_dma=True)
        pss = PSB[0]
        for k in range(16):
            P.op("act", lambda e, k=k: e.activation(sqt[:], ymix[:, k * T:(k + 1) * T], AF.Square), [ymix], [sqt])
            P.op("pe", lambda e, k=k: e.matmul(pss[:, 0:T], CS("ones"), sqt[:], start=(k == 0), stop=(k == 15)), [cst, sqt], [pss])
        P.op("dve", lambda e: e.tensor_scalar(rbc[:], pss[:, 0:T], 1.0 / 2048, EPS, ALU.mult, ALU.add), [pss], [rbc])
        P.op("act", lambda e: e.activation(rbc[:], rbc[:], AF.Sqrt), [rbc], [rbc])
        P.op("dve", lambda e: e.reciprocal(rbc[:], rbc[:]), [rbc], [rbc])
        for k in range(32):
            if k < 16:
                P.op("dve", lambda e, k=k: e.scalar_tensor_tensor(Mb[:, k, :], ymix[:, k * T:(k + 1) * T], lng[:, k:k + 1], rbc[:], ALU.mult, ALU.mult),
                     [ymix, lng, rbc, PR["xn"]], [Mb, PR["xn"]])
            else:
                P.op("pool", lambda e, k=k: e.tensor_copy(Mb[:, k, :], ymix[:, k * T:(k + 1) * T]), [ymix, PR["xn"]], [Mb, PR["xn"]])
        for nh in range(2):
            for k in range(32):
                ws, wb = wsl[wj % 2], wslb[wj % 2]
                wj += 1
                P.dma(ws[:], wout_d[k * 128:(k + 1) * 128, nh * 2048:(nh + 1) * 2048], writes=[ws])
                cast(wb, wb[:], ws, ws[:])
                for tb in range(TB):
                    for n4 in range(4):
                        pb = PSB[tb * 4 + n4]
                        P.op("pe", lambda e, k=k, tb=tb, n4=n4, pb=pb, wb=wb: e.matmul(pb[:, :], Mb[:, k, tb * 128:(tb + 1) * 128], wb[:, n4 * 512:(n4 + 1) * 512],
                                                                                   start=(k == 0), stop=(k == 31)), [Mb, wb], [pb])
            for tb in range(TB):
                for n4 in range(4):
                    pb = PSB[tb * 4 + n4]
                    c0 = tb * D + nh * 2048 + n4 * 512
                    evac(R2, R2[:, c0:c0 + 512], pb, pb[:, :])
        for tb in range(TB):
            osl = tok_epilogue(R2, tb, 0)
            xs = R1[:, tb * D:(tb + 1) * D]
            P.dma(xs, xq_d[t0 + tb * 128: t0 + (tb + 1) * 128, :], writes=[R1])
            P.op("pool", lambda e, xs=xs, osl=osl: e.tensor_tensor(osl, osl, xs, ALU.add), [R2, R1], [R2])
        for tb in range(TB):
            xn = PR["xn"]
            src_ap = R2[:, tb * D:(tb + 1) * D]
            P.op("act", lambda e, tb=tb, src_ap=src_ap: e.activation(xn[:, tb, :], src_ap, AF.Square, accum_out=st1[:, tb:tb + 1]), [R2, Mb], [xn, Mb, st1])
            rstd_from_ss(st1[:, tb:tb + 1], st1[:, 4 + tb:5 + tb], D, [st1], [st1])
            P.op("dve", lambda e, tb=tb, src_ap=src_ap: e.tensor_scalar(xn[:, tb, :], src_ap, st1[:, 4 + tb:5 + tb], None, ALU.mult), [R2, st1, Mb], [xn, Mb])
        make_hT(TB, 0, 1)
        for j in range(86):
            pg, pu = PSB[(2 * j) % 8], PSB[(2 * j + 1) % 8]
            for kh in range(2):
                for (wsrc, pacc, si) in ((wg_d, pg, 0), (wu_d, pu, 1)):
                    ws, wb = wsl[si], wslb[si]
                    wsv = ws[:].rearrange("p (k n) -> p k n", n=128)
                    wbv = wb[:].rearrange("p (k n) -> p k n", n=128)
                    P.dma(wsv, wsrc[kh * 2048:(kh + 1) * 2048, j * 128:(j + 1) * 128].rearrange("(k p) n -> p k n", p=128), writes=[ws])
                    cast(wb, wb[:], ws, ws[:])
                    for kk in range(16):
                        k = kh * 16 + kk
                        P.op("pe", lambda e, k=k, kk=kk, pacc=pacc, wbv=wbv: e.matmul(pacc[:, 0:T], wbv[:, kk, :], hT[k][:, 0:T], start=(k == 0), stop=(k == 31)),
                             [wb, hT[k]], [pacc])
            P.op("act", lambda e, pg=pg: e.activation(sil[:], pg[:, 0:T], AF.Silu), [pg], [sil])
            P.op("dve", lambda e, j=j, pu=pu: e.tensor_tensor(aT[j][:], pu[:, 0:T], sil[:], ALU.mult), [pu, sil], [aT[j]])
        for nh in range(2):
            for k in range(86):
                ws, wb = wsl[wj % 2], wslb[wj % 2]
                wj += 1
                P.dma(ws[:], wd_d[k * 128:(k + 1) * 128, nh * 2048:(nh + 1) * 2048], writes=[ws])
                cast(wb, wb[:], ws, ws[:])
                for tb in range(TB):
                    for n4 in range(4):
                        pb = PSB[tb * 4 + n4]
                        P.op("pe", lambda e, k=k, tb=tb, n4=n4, pb=pb, wb=wb: e.matmul(pb[:, :], aT[k][:, tb * 128:(tb + 1) * 128], wb[:, n4 * 512:(n4 + 1) * 512],
                                                                                   start=(k == 0), stop=(k == 85)), [aT[k], wb], [pb])
            for tb in range(TB):
                for n4 in range(4):
                    pb = PSB[tb * 4 + n4]
                    c0 = tb * D + nh * 2048 + n4 * 512
                    evac(R1, R1[:, c0:c0 + 512], pb, pb[:, :])
        for tb in range(TB):
            osl = tok_epilogue(R1, tb, 1)
            P.op("pool", lambda e, tb=tb, osl=osl: e.tensor_tensor(osl, osl, R2[:, tb * D:(tb + 1) * D], ALU.add), [R1, R2], [R1])
            P.dma(out_d[t0 + tb * 128: t0 + (tb + 1) * 128, :], osl, reads=[R1], writes=[out_d])

    P.emit(final_wait_bufs=[out_d] + ([mod_d, p_bufs[2], y_d] if dbg else []))
    return nc


def kernel(**inp):
    f = lambda a: np.ascontiguousarray(np.asarray(a, dtype=np.float32))
    if "_dbg" in inp:
        globals()["NCORES"] = 1
    x = f(inp["x"]); c = f(inp["c"]); ctx = f(inp["ctx"]); c_ctx = f(inp["c_ctx"])
    L = 0
    gvecs = np.concatenate([f(inp["g_pre_mix"])[L].reshape(32, 128).T, f(inp["g_pre_ffn"])[L].reshape(32, 128).T], axis=1)
    grows = np.stack([f(inp["g_post_mix"])[L], f(inp["g_post_ffn"])[L]], 0)
    lcw = np.concatenate([f(inp["lru_conv_w"])[L], f(inp["lru_conv_b"])[L][None]], 0)
    lcw = lcw.reshape(5, 16, 128).transpose(2, 1, 0).reshape(128, 80)
    lv = np.stack([f(inp["lru_b_a"])[L], f(inp["lru_b_x"])[L], f(inp["lru_lambda"])[L]], -1)
    lv = lv.reshape(2, 16, 128, 3).transpose(2, 0, 1, 3).reshape(128, 96)
    lng = f(inp["lru_norm_g"])[L].reshape(16, 128).T
    gcw = f(inp["gdn_conv_w"])[L].reshape(4, 48, 128).transpose(2, 1, 0).reshape(128, 192)
    gsc = np.concatenate([f(inp["gdn_a_log"])[L].reshape(-1), f(inp["gdn_dt_bias"])[L].reshape(-1)])[None]
    gng = f(inp["gdn_norm_g"])[L].reshape(128, 1)
    common = {
        "w_ada": f(inp["w_ada"])[L], "b_ada": f(inp["b_ada"])[L][None] if f(inp["b_ada"]).ndim == 2 else f(inp["b_ada"]),
        "gvecs": f(gvecs), "grows": f(grows), "w_in": f(inp["w_in"])[L], "lru_cw": f(lcw),
        "lru_w_a": f(inp["lru_w_a"])[L].reshape(2 * 16 * 128, 128), "lru_w_x": f(inp["lru_w_x"])[L].reshape(2 * 16 * 128, 128),
        "lru_vecs": f(lv), "lru_ng": f(lng), "gdn_cw": f(gcw), "gdn_sc": f(gsc), "gdn_ng": f(gng),
        "w_out": f(inp["w_out"])[L], "w_g": f(inp["w_ffn_gate"])[L], "w_u": f(inp["w_ffn_up"])[L], "w_d": f(inp["w_ffn_down"])[L],
        "consts": CONST_ARR,
    }
    common["b_ada"] = f(inp["b_ada"])[L].reshape(1, -1)
    in_maps = []
    for core in range(NCORES):
        b, q = (core // 4, core % 4) if NCORES == 8 else (core, 0)
        m = dict(common)
        m["x"] = x[b]
        m["ctx"] = ctx[b]
        m["cin"] = f(np.concatenate([c[b].reshape(32, 128), c_ctx.reshape(32, 128)], 0))
        m["x_q"] = np.ascontiguousarray(x[b, q * TOKQ:(q + 1) * TOKQ])
        ti = np.arange(8)[None, :, None]; kk = np.arange(32)[None, None, :]; pp = np.arange(128)[:, None, None]
        m["yidx"] = np.ascontiguousarray(((q * 8 + ti) * D + kk * 128 + pp).reshape(128, 256).astype(np.uint32))
        in_maps.append(m)
    if "_dbg" in inp:
        nc = build(stop_after=inp["_dbg"], dbg=True)
        return run_bass_kernel_spmd(nc, in_maps, core_ids=list(range(NCORES)))
    nc = build()
    res = run_bass_kernel_spmd(nc, in_maps, core_ids=list(range(NCORES)))
    out = np.stack([res.results[c_]["out"] for c_ in range(NCORES)], 0).reshape(2, SEQ, D)
    return out.astype(np.float32)
```

```python
import contextlib
import numpy as np
import concourse.bass as bass
import concourse.mybir as mybir
from concourse.bass_utils import run_bass_kernel_spmd

F32 = mybir.dt.float32
BF16 = mybir.dt.bfloat16
AF = mybir.ActivationFunctionType
ALU = mybir.AluOpType

D = 4096
SEQ = 8192
CTX = 256
NT = SEQ + CTX
NIN = 12352
DFF = 11008
EPS = 1e-6
COL_G = 2048
COL_QKV = 4096
COL_Z = 4096 + 6144
COL_BETA = COL_Z + 2048
COL_ALPHA = COL_BETA + 32
NCORES = 8
TOKQ = SEQ // 4
DBG = {}


class Buf:
    def __init__(self, h, name):
        self.h = h
        self.name = name
        self.lw = None
        self.rd = []

    def __getitem__(self, idx):
        return self.h[idx]


import types


def _freeze(fn):
    if fn is None or fn.__closure__ is None:
        return fn
    cells = []
    for c in fn.__closure__:
        try:
            cells.append(types.CellType(c.cell_contents))
        except ValueError:
            cells.append(c)
    return types.FunctionType(fn.__code__, fn.__globals__, fn.__name__, fn.__defaults__, tuple(cells))


class Sub:
    def __init__(self, parent, ap):
        self.p = parent
        self.h = ap
        self.name = parent.name

    def __getitem__(self, idx):
        return self.h[idx]

    @property
    def lw(self):
        return self.p.lw

    @lw.setter
    def lw(self, v):
        self.p.lw = v

    @property
    def rd(self):
        return self.p.rd

    @rd.setter
    def rd(self, v):
        self.p.rd = v


class Op:
    __slots__ = ("eng", "fn", "deps", "is_dma", "need_inc", "sem", "val", "idx")


class Prog:
    ENGS = ("sync", "pe", "act", "dve", "pool")

    ARENA = 48640

    def __init__(self, nc):
        self.nc = nc
        self.ops = []
        self.last = {}
        self.dcnt = {}
        self.dlast = {}
        self.arena = None
        self.aptr = 0

    def perm(self, name, shape, dt=F32):
        return Buf(self.nc.alloc_sbuf_tensor(name, list(shape), dt), name)

    def sb(self, name, shape, dt=F32):
        if self.arena is None:
            self.arena = self.nc.alloc_sbuf_tensor("arena", [128, self.ARENA], F32)
        nel = 1
        for v in shape[1:]:
            nel *= v
        is4 = dt in (F32, mybir.dt.uint32, mybir.dt.int32)
        nfl = nel if is4 else (nel + 1) // 2
        nfl = (nfl + 7) // 8 * 8
        a = self.aptr
        self.aptr += nfl
        assert self.aptr <= self.ARENA, ("arena overflow", name, self.aptr)
        ap = self.arena[0:shape[0], a:a + nfl]
        if dt != F32:
            ap = ap.bitcast(dt)
        ap = ap[:, 0:nel]
        if len(shape) == 3:
            ap = ap.rearrange("p (a b) -> p a b", b=shape[2])
        return Buf(ap, name)

    def fence(self):
        deps = set(self.last.values()) | set(self.dlast.values())
        for e in self.ENGS:
            o = self.op(e, None)
            o.deps = sorted(deps)

    def barrier(self):
        deps = set(self.last.values()) | set(self.dlast.values())
        for e in self.ENGS:
            o = self.op(e, None)
            o.deps = sorted(deps)
        self.aptr = 0

    def ps(self, name, shape, dt=F32):
        return Buf(self.nc.alloc_psum_tensor(name, list(shape), dt), name)

    def dram(self, name, shape, dt=F32, kind="Internal"):
        return Buf(self.nc.dram_tensor(name, list(shape), dt, kind=kind).ap(), name)

    def op(self, eng, fn, reads=(), writes=(), is_dma=False):
        o = Op()
        o.eng = eng
        o.fn = _freeze(fn)
        o.is_dma = is_dma
        o.need_inc = is_dma
        o.sem = None
        o.val = 0
        o.idx = len(self.ops)
        deps = set()
        for b in reads:
            if b.lw is not None:
                deps.add(b.lw)
        for b in writes:
            if b.lw is not None:
                deps.add(b.lw)
            deps.update(b.rd)
        ops = self.ops
        o.deps = sorted(d for d in deps
                        if not (eng == "pe" and ops[d].eng == "pe" and not ops[d].is_dma))
        ops.append(o)
        if fn is not None:
            self.last[eng] = o.idx
        if is_dma:
            k = self.dcnt.get(eng, 0)
            self.dcnt[eng] = k + 1
            prev = self.dlast.get((eng, k % 8))
            if prev is not None:
                o.deps = sorted(set(o.deps) | {prev})
            self.dlast[(eng, k % 8)] = o.idx
            o.val = (k % 8, 16 * (k // 8 + 1))
        for b in writes:
            b.lw = o.idx
            b.rd = []
        for b in reads:
            if b.lw != o.idx:
                b.rd.append(o.idx)
        return o

    def dma(self, out_ap, in_ap, reads=(), writes=(), q="sync", **kw):
        return self.op(q, lambda e: e.dma_start(out=out_ap, in_=in_ap, **kw), reads, writes, is_dma=True)

    def emit(self, final_wait_bufs=()):
        nc = self.nc
        ops = self.ops
        self.op("sync", None, reads=list(final_wait_bufs))
        for o in ops:
            for d in o.deps:
                ops[d].need_inc = True
        NDMA = 8
        with contextlib.ExitStack() as st:
            esem = {e: st.enter_context(nc.semaphore("s_" + e)) for e in self.ENGS}
            dsem = {e: [st.enter_context(nc.semaphore("d_%s%d" % (e, i))) for i in range(NDMA)]
                    for e in ("sync", "act", "pool")}
            ecnt = {e: 0 for e in self.ENGS}
            for o in ops:
                if o.is_dma:
                    k, v = o.val
                    o.sem = dsem[o.eng][k]
                    o.val = v
                elif o.need_inc:
                    ecnt[o.eng] += 1
                    o.sem = esem[o.eng]
                    o.val = ecnt[o.eng]
            per = {e: [o for o in ops if o.eng == e] for e in self.ENGS}

            def run(engname, eng):
                seen = {}
                for o in per[engname]:
                    for d in o.deps:
                        p = ops[d]
                        key = id(p.sem)
                        if seen.get(key, 0) >= p.val:
                            continue
                        eng.wait_ge(p.sem, p.val)
                        seen[key] = p.val
                    if o.fn is None:
                        continue
                    ins = o.fn(eng)
                    if o.sem is not None:
                        ins.then_inc(o.sem, 16 if o.is_dma else 1)

            with nc.Block() as block:
                @block.sync
                def _(e):
                    run("sync", e)

                @block.tensor
                def _(e):
                    run("pe", e)

                @block.scalar
                def _(e):
                    run("act", e)

                @block.vector
                def _(e):
                    run("dve", e)

                @block.gpsimd
                def _(e):
                    run("pool", e)


def make_consts():
    i = np.arange(128)
    same = (i[:, None] // 64) == (i[None, :] // 64)
    r = i[:, None]
    c = i[None, :]
    C = {}
    C["ident"] = np.eye(128)
    C["ones"] = np.ones((128, 128))
    NEG = -30000.0
    C["maskT0"] = np.where(same & (c >= r), 0.0, NEG)
    C["maskT1"] = np.where(same & (c <= r), 0.0, NEG)
    C["pos0"] = np.where(same & (r > c), 0.0, -NEG)
    C["pos1"] = np.where(same & (r < c), 0.0, -NEG)
    C["ucum0"] = (same & (r <= c)).astype(np.float64)
    C["ucum1"] = (same & (r >= c)).astype(np.float64)
    C["bsame"] = same.astype(np.float64)
    C["selA"] = np.broadcast_to((r < 64), (128, 128)).astype(np.float64)
    C["selB"] = np.broadcast_to((r >= 64), (128, 128)).astype(np.float64)
    names = list(C.keys())
    arr = np.concatenate([C[n] for n in names], axis=1).astype(np.float32)
    return names, arr


CONST_NAMES, CONST_ARR = make_consts()


def build(stop_after=99, dbg=False):
    nc = bass.Bass("TRN2", target_bir_lowering=False)
    P = Prog(nc)
    EI = "ExternalInput"
    x_d = P.dram("x", [SEQ, D], F32, EI)
    ctx_d = P.dram("ctx", [CTX, D], F32, EI)
    cin_d = P.dram("cin", [64, 128], F32, EI)
    wada_d = P.dram("w_ada", [D, 6 * D], F32, EI)
    bada_d = P.dram("b_ada", [1, 6 * D], F32, EI)
    gv_d = P.dram("gvecs", [128, 64], F32, EI)
    grow_d = P.dram("grows", [2, D], F32, EI)
    win_d = P.dram("w_in", [D, NIN], F32, EI)
    lcw_d = P.dram("lru_cw", [128, 16 * 5], F32, EI)
    lwa_d = P.dram("lru_w_a", [2 * 16 * 128, 128], F32, EI)
    lwx_d = P.dram("lru_w_x", [2 * 16 * 128, 128], F32, EI)
    lv_d = P.dram("lru_vecs", [128, 2 * 16 * 3], F32, EI)
    lng_d = P.dram("lru_ng", [128, 16], F32, EI)
    gcw_d = P.dram("gdn_cw", [128, 48 * 4], F32, EI)
    gsc_d = P.dram("gdn_sc", [1, 64], F32, EI)
    gng_d = P.dram("gdn_ng", [128, 1], F32, EI)
    wout_d = P.dram("w_out", [D, D], F32, EI)
    wg_d = P.dram("w_g", [D, DFF], F32, EI)
    wu_d = P.dram("w_u", [D, DFF], F32, EI)
    wd_d = P.dram("w_d", [DFF, D], F32, EI)
    cst_d = P.dram("consts", [128, CONST_ARR.shape[1]], F32, EI)
    xq_d = P.dram("x_q", [TOKQ, D], F32, EI)
    yidx_d = P.dram("yidx", [128, 256], mybir.dt.uint32, EI)
    out_d = P.dram("out", [TOKQ, D], F32, "ExternalOutput")
    IK = "ExternalOutput" if dbg else "Internal"
    mod_d = P.dram("mod_s", [2, 6 * D], F32, IK)
    p_parts = [(0, 4096), (4096, 10240), (10240, NIN)]
    p_bufs = [P.dram("p_s%d" % i, [b - a, NT], F32, IK if i == 2 else "Internal") for i, (a, b) in enumerate(p_parts)]

    class PD:
        def rows(self, r0, r1):
            for (a, b), buf in zip(p_parts, p_bufs):
                if a <= r0 and r1 <= b:
                    return buf, buf[r0 - a:r1 - a, :]
            raise AssertionError((r0, r1))
    pd = PD()
    y_d = P.dram("y_s", [32 * D, 256], F32, IK)
    y_v = y_d[:].rearrange("(tt c) t -> c tt t", c=D)
    qT_d = P.dram("qT_s", [128, NT], F32)
    kT_d = P.dram("kT_s", [128, NT], F32)
    kt_d = P.dram("kt_s", [NT, 128], F32)
    vt_d = P.dram("vt_s", [NT, 128], F32)

    cst = P.perm("cst", [128, CONST_ARR.shape[1]])
    P.dma(cst[:], cst_d[:], writes=[cst])

    def CS(name):
        k = CONST_NAMES.index(name)
        return cst[:, k * 128:(k + 1) * 128]
    identb = P.perm("identb", [128, 128], BF16)
    P.op("dve", lambda e: e.tensor_copy(identb[:], CS("ident")), [cst], [identb])

    PSB = [P.ps("psb%d" % i, [128, 512]) for i in range(8)]

    cc = P.sb("cc", [64, 128])
    sT = P.sb("sT", [128, 64])
    P.dma(cc[:], cin_d[:], writes=[cc])
    P.op("act", lambda e: e.activation(cc[:], cc[:], AF.Silu), [cc], [cc])
    P.op("pe", lambda e: e.transpose(PSB[0][:, 0:64], cc[:], CS("ident")[0:64, 0:64]), [cc, cst], [PSB[0]])
    P.op("act", lambda e: e.activation(sT[:], PSB[0][:, 0:64], AF.Copy), [PSB[0]], [sT])
    wa = [P.sb("wa%d" % i, [128, 3072]) for i in range(2)]
    mrow = P.sb("mrow", [2, 3072])
    brow = P.sb("brow", [2, 3072])
    it = 0
    for g in range(8):
        c0 = g * 3072
        P.dma(brow[0:1, :], bada_d[0:1, c0:c0 + 3072], writes=[brow])
        P.dma(brow[1:2, :], bada_d[0:1, c0:c0 + 3072], writes=[brow])
        for k in range(32):
            w = wa[it % 2]
            it += 1
            P.dma(w[:], wada_d[k * 128:(k + 1) * 128, c0:c0 + 3072], writes=[w])
            for n in range(6):
                P.op("pe", lambda e, w=w, n=n, k=k: e.matmul(PSB[n][0:2, :], sT[:, k:64:32], w[:, n * 512:(n + 1) * 512],
                                                          start=(k == 0), stop=(k == 31)), [sT, w], [PSB[n]])
        for n in range(6):
            P.op("dve", lambda e, n=n: e.tensor_tensor(mrow[:, n * 512:(n + 1) * 512], PSB[n][0:2, :], brow[:, n * 512:(n + 1) * 512], ALU.add),
                 [PSB[n], brow], [mrow])
        P.dma(mod_d[:, c0:c0 + 3072], mrow[:], reads=[mrow], writes=[mod_d])
    modT = P.perm("modT", [128, 2 * 192])
    mld = P.sb("mld", [96, 128])
    for j in range(2):
        for h in range(2):
            src = mod_d[j:j + 1, h * 96 * 128:(h + 1) * 96 * 128].rearrange("o (q p) -> (o q) p", p=128)
            P.dma(mld[:], src, reads=[mod_d], writes=[mld])
            P.op("pe", lambda e: e.transpose(PSB[0][:, 0:96], mld[:], CS("ident")[0:96, 0:96]), [mld, cst], [PSB[0]])
            P.op("act", lambda e, j=j, h=h: e.activation(modT[:, j * 192 + h * 96: j * 192 + (h + 1) * 96], PSB[0][:, 0:96], AF.Copy),
                 [PSB[0]], [modT])
    gv = P.perm("gv", [128, 64])
    P.dma(gv[:], gv_d[:], writes=[gv])
    GS = P.perm("GS", [128, 4 * 32])
    for idx, (j, which) in enumerate(((0, 0), (1, 0), (0, 1))):
        sc0 = j * 192 + (1 + 3 * which) * 32
        P.op("dve", lambda e, idx=idx, sc0=sc0, which=which: e.scalar_tensor_tensor(
            GS[:, idx * 32:(idx + 1) * 32], modT[:, sc0:sc0 + 32], 1.0, gv[:, which * 32:(which + 1) * 32], ALU.add, ALU.mult),
            [modT, gv], [GS])

    def SH(j, which, k):
        c = j * 192 + (3 * which) * 32 + k
        return modT[:, c:c + 1]

    def GSc(j, which, k):
        idx = {(0, 0): 0, (1, 0): 1, (0, 1): 2}[(j, which)]
        return GS[:, idx * 32 + k: idx * 32 + k + 1]

    st1 = P.perm("st1", [128, 8])
    st2 = P.perm("st2", [128, 8])
    lng = P.perm("lng", [128, 16])
    gng = P.perm("gng", [128, 1])
    PR = {}

    def rstd_from_ss(ss_ap, out_ap, n, bufs_r, bufs_w):
        P.op("dve", lambda e: e.tensor_scalar(out_ap, ss_ap, 1.0 / n, EPS, ALU.mult, ALU.add), bufs_r, bufs_w)
        P.op("act", lambda e: e.activation(out_ap, out_ap, AF.Sqrt), bufs_w, bufs_w)
        P.op("dve", lambda e: e.reciprocal(out_ap, out_ap), bufs_w, bufs_w)

    def norm_rows(src_buf, src_ap, tb):
        xn = PR["xn"]
        P.op("act", lambda e: e.activation(xn[:, tb, :], src_ap, AF.Square, accum_out=st1[:, tb:tb + 1]), [src_buf], [xn, st1])
        rstd_from_ss(st1[:, tb:tb + 1], st1[:, 4 + tb:5 + tb], D, [st1], [st1])
        P.op("dve", lambda e: e.tensor_scalar(xn[:, tb, :], src_ap, st1[:, 4 + tb:5 + tb], None, ALU.mult), [src_buf, st1], [xn])

    def make_hT(ntb, j, which):
        xn, hT = PR["xn"], PR["hT"]
        for k in range(32):
            pb = PSB[k % 2]
            pbv = pb[:].bitcast(BF16)
            for tb in range(ntb):
                P.op("pe", lambda e, k=k, tb=tb, pbv=pbv: e.transpose(pbv[:, tb * 128:(tb + 1) * 128], xn[:, tb, k * 128:(k + 1) * 128], identb[:]),
                     [xn, identb], [pb])
            P.op("act", lambda e, k=k, pbv=pbv: e.activation(hT[k][:, 0:ntb * 128], pbv[:, 0:ntb * 128], AF.Identity,
                                                          scale=GSc(j, which, k), bias=SH(j, which, k)), [pb, GS, modT], [hT[k]])

    P.barrier()
    xrow = [P.sb("xrow%d" % i, [128, D]) for i in range(2)]
    PR["xn"] = P.sb("xn", [128, 4, D], BF16)
    hT = [P.sb("hT%d" % k, [128, 512], BF16) for k in range(32)]
    PR["hT"] = hT
    wst = [P.sb("wst%d" % i, [128, 32, 128]) for i in range(2)]
    wbf = [P.sb("wbf%d" % i, [128, 32, 128], BF16) for i in range(2)]
    pst = [P.sb("pst%d" % i, [128, 512]) for i in range(2)]
    NCH = (NIN + 127) // 128
    tiles = [(t * 512, 4, 0) for t in range(16)] + [(SEQ, 2, 1)]
    if DBG.get('skip1'):
        tiles = []
    wi = 0
    for (t0, ntb, j) in tiles:
        for tb in range(ntb):
            xr = xrow[tb % 2]
            src = x_d[t0 + tb * 128: t0 + (tb + 1) * 128, :] if j == 0 else ctx_d[tb * 128:(tb + 1) * 128, :]
            P.dma(xr[:], src, writes=[xr])
            norm_rows(xr, xr[:], tb)
        make_hT(ntb, j, 0)
        T = ntb * 128
        for c in range(NCH):
            m = min(128, NIN - c * 128)
            ws, wb = wst[wi % 2], wbf[wi % 2]
            wi += 1
            P.dma(ws[:, :, 0:m], win_d[:, c * 128:c * 128 + m].rearrange("(k p) n -> p k n", p=128), writes=[ws])
            P.op("pool", lambda e, ws=ws, wb=wb, m=m: e.tensor_copy(wb[:, :, 0:m], ws[:, :, 0:m]), [ws], [wb])
            pb = PSB[2 + c % 4]
            for k in range(32):
                P.op("pe", lambda e, k=k, wb=wb, pb=pb, m=m, T=T: e.matmul(pb[0:m, 0:T], wb[:, k, 0:m], hT[k][:, 0:T], start=(k == 0), stop=(k == 31)),
                     [wb, hT[k]], [pb])
            po = pst[c % 2]
            eng = "act" if c % 2 == 0 else "dve"
            if eng == "act":
                P.op("act", lambda e, po=po, pb=pb, m=m, T=T: e.activation(po[0:m, 0:T], pb[0:m, 0:T], AF.Copy), [pb], [po])
            else:
                P.op("dve", lambda e, po=po, pb=pb, m=m, T=T: e.tensor_copy(po[0:m, 0:T], pb[0:m, 0:T]), [pb], [po])
            pbuf, pap = pd.rows(c * 128, c * 128 + m)
            P.dma(pap[:, t0:t0 + T], po[0:m, 0:T], reads=[po], writes=[pbuf])

    if stop_after < 2:
        P.emit(final_wait_bufs=[mod_d, p_bufs[2]])
        return nc
    P.barrier()
    W = NT
    big = [P.sb("big%d" % i, [128, W]) for i in range(4)]
    lcw = P.sb("lcw", [128, 80])
    lv = P.sb("lv", [128, 96])
    P.dma(lcw[:], lcw_d[:], writes=[lcw])
    P.dma(lv[:], lv_d[:], writes=[lv])
    P.dma(lng[:], lng_d[:], writes=[lng])
    lcs = P.sb("lcs", [128, 64])
    lam_v = lv[:, 2:96:3]
    P.op("act", lambda e: e.activation(lcs[:, 0:32], lam_v, AF.Exp, scale=-1.0), [lv], [lcs])
    P.op("dve", lambda e: e.tensor_scalar(lcs[:, 0:32], lcs[:, 0:32], 1.0, None, ALU.add), [lcs], [lcs])
    P.op("act", lambda e: e.activation(lcs[:, 0:32], lcs[:, 0:32], AF.Ln), [lcs], [lcs])
    P.op("dve", lambda e: e.tensor_scalar(lcs[:, 32:64], lcs[:, 0:32], -16.0, None, ALU.mult), [lcs], [lcs])
    P.op("dve", lambda e: e.tensor_scalar(lcs[:, 0:32], lcs[:, 0:32], -8.0, None, ALU.mult), [lcs], [lcs])
    lw = [P.sb("lw%d" % i, [128, 128]) for i in range(4)]
    tmpc = [P.sb("tmpc%d" % i, [128, 512]) for i in range(6)]

    def conv_seg(dst, src, a, b, wt, wcol, bias_ap):
        if bias_ap is not None:
            P.op("dve", lambda e: e.tensor_scalar(dst[:, a:b], src[:, a:b], wt[:, wcol + 2:wcol + 3], bias_ap, ALU.mult, ALU.add), [src, wt], [dst])
        else:
            P.op("dve", lambda e: e.tensor_scalar(dst[:, a:b], src[:, a:b], wt[:, wcol + 2:wcol + 3], None, ALU.mult), [src, wt], [dst])
        for (jj, off) in ((0, -2), (1, -1), (3, 1)):
            if off < 0:
                da, db, sa, sb_ = a - off, b, a, b + off
            else:
                da, db, sa, sb_ = a, b - off, a + off, b
            P.op("dve", lambda e, jj=jj, da=da, db=db, sa=sa, sb_=sb_: e.scalar_tensor_tensor(
                dst[:, da:db], src[:, sa:sb_], wt[:, wcol + jj:wcol + jj + 1], dst[:, da:db], ALU.mult, ALU.add), [src, wt, dst], [dst])

    def gelu_inplace(buf, a, b, tmp):
        P.op("pool", lambda e: e.tensor_tensor(tmp[:, a:b], buf[:, a:b], buf[:, a:b], ALU.mult), [buf], [tmp])
        P.op("pool", lambda e: e.tensor_scalar(tmp[:, a:b], tmp[:, a:b], 0.044715, 1.0, ALU.mult, ALU.add), [tmp], [tmp])
        P.op("pool", lambda e: e.tensor_tensor(tmp[:, a:b], tmp[:, a:b], buf[:, a:b], ALU.mult), [tmp, buf], [tmp])
        P.op("act", lambda e: e.activation(tmp[:, a:b], tmp[:, a:b], AF.Sigmoid, scale=1.5957691216), [tmp], [tmp])
        P.op("pool", lambda e: e.tensor_tensor(buf[:, a:b], buf[:, a:b], tmp[:, a:b], ALU.mult), [buf, tmp], [buf])

    for n in range(0 if DBG.get('skip2') else 16):
        u, gt, xl, hf = big
        pbuf, pap = pd.rows(n * 128, (n + 1) * 128)
        P.dma(u[:], pap, reads=[pbuf], writes=[u])
        pbuf, pap = pd.rows(COL_G + n * 128, COL_G + (n + 1) * 128)
        P.dma(gt[:, 0:SEQ], pap[:, 0:SEQ], reads=[pbuf], writes=[gt])
        conv_seg(xl, u, 0, SEQ, lcw, n * 5, lcw[:, n * 5 + 4:n * 5 + 5])
        conv_seg(xl, u, SEQ, NT, lcw, n * 5, lcw[:, n * 5 + 4:n * 5 + 5])
        gelu_inplace(gt, 0, SEQ, u)
        for d in range(2):
            dn = d * 16 + n
            P.dma(lw[2 * d][:], lwa_d[dn * 128:(dn + 1) * 128, :], writes=[lw[2 * d]])
            P.dma(lw[2 * d + 1][:], lwx_d[dn * 128:(dn + 1) * 128, :], writes=[lw[2 * d + 1]])
            chunks = [(SEQ, NT)] + [(c * 512, (c + 1) * 512) for c in range(16)]
            if d == 1:
                chunks = [(SEQ, NT)] + [(c * 512, (c + 1) * 512) for c in range(15, -1, -1)]
            prev = None
            for ci, (a, b) in enumerate(chunks):
                L = b - a
                pa, px = PSB[(2 * ci) % 8], PSB[(2 * ci + 1) % 8]
                P.op("pe", lambda e, pa=pa, a=a, b=b, L=L: e.matmul(pa[:, 0:L], lw[2 * d][:], xl[:, a:b], start=True, stop=True), [lw[2 * d], xl], [pa])
                P.op("pe", lambda e, px=px, a=a, b=b, L=L: e.matmul(px[:, 0:L], lw[2 * d + 1][:], xl[:, a:b], start=True, stop=True), [lw[2 * d + 1], xl], [px])
                r_, i_, a_, a2_, b_, hb_ = tmpc
                P.op("act", lambda e, pa=pa, L=L: e.activation(r_[:, 0:L], pa[:, 0:L], AF.Sigmoid, bias=lv[:, dn * 3:dn * 3 + 1]), [pa, lv], [r_])
                P.op("act", lambda e, px=px, L=L: e.activation(i_[:, 0:L], px[:, 0:L], AF.Sigmoid, bias=lv[:, dn * 3 + 1:dn * 3 + 2]), [px, lv], [i_])
                P.op("act", lambda e, L=L: e.activation(a_[:, 0:L], r_[:, 0:L], AF.Exp, scale=lcs[:, dn:dn + 1]), [r_, lcs], [a_])
                P.op("act", lambda e, L=L: e.activation(a2_[:, 0:L], r_[:, 0:L], AF.Exp, scale=lcs[:, 32 + dn:33 + dn]), [r_, lcs], [a2_])
                P.op("dve", lambda e, L=L: e.tensor_scalar(a2_[:, 0:L], a2_[:, 0:L], -1.0, 1.0, ALU.mult, ALU.add), [a2_], [a2_])
                P.op("dve", lambda e, L=L: e.tensor_scalar(a2_[:, 0:L], a2_[:, 0:L], 1e-30, None, ALU.max), [a2_], [a2_])
                P.op("act", lambda e, L=L: e.activation(a2_[:, 0:L], a2_[:, 0:L], AF.Sqrt), [a2_], [a2_])
                P.op("pool", lambda e, a=a, b=b, L=L: e.tensor_tensor(i_[:, 0:L], i_[:, 0:L], xl[:, a:b], ALU.mult), [i_, xl], [i_])
                P.op("pool", lambda e, L=L: e.tensor_tensor(b_[:, 0:L], i_[:, 0:L], a2_[:, 0:L], ALU.mult), [i_, a2_], [b_])
                if d == 0:
                    init = 0.0 if prev is None else hf[:, prev:prev + 1]
                    P.op("dve", lambda e, a=a, b=b, L=L, init=init: e.tensor_tensor_scan(hf[:, a:b], a_[:, 0:L], b_[:, 0:L], init, ALU.mult, ALU.add),
                         [a_, b_, hf], [hf])
                    prev = b - 1
                else:
                    init = 0.0 if prev is None else u[:, prev:prev + 1]
                    P.op("dve", lambda e, a=a, b=b, L=L, init=init: e.tensor_tensor_scan(u[:, a:b][:, ::-1], a_[:, 0:L][:, ::-1], b_[:, 0:L][:, ::-1], init, ALU.mult, ALU.add),
                         [a_, b_, u], [u])
                    prev = a
                    if a < SEQ:
                        P.op("pool", lambda e, a=a, b=b, L=L: e.tensor_tensor(hb_[:, 0:L], u[:, a:b], hf[:, a:b], ALU.add), [u, hf], [hb_])
                        P.op("pool", lambda e, a=a, b=b, L=L: e.tensor_tensor(hb_[:, 0:L], hb_[:, 0:L], gt[:, a:b], ALU.mult), [hb_, gt], [hb_])
                        P.dma(y_v[n * 128:(n + 1) * 128, a // 256:b // 256, :], hb_[:, 0:L].rearrange("p (a t) -> p a t", t=256), reads=[hb_], writes=[y_d])

    if stop_after < 3:
        P.emit(final_wait_bufs=[mod_d, p_bufs[2], y_d])
        return nc
    P.barrier()
    big = [P.sb("big%d" % i, [128, W]) for i in range(4)]
    gcw = P.sb("gcw", [128, 192])
    P.dma(gcw[:], gcw_d[:], writes=[gcw])
    gsc = P.sb("gsc", [128, 64])
    P.dma(gsc[:], gsc_d[0:1, :].partition_broadcast(128), writes=[gsc])
    P.op("act", lambda e: e.activation(gsc[:, 0:32], gsc[:, 0:32], AF.Exp), [gsc], [gsc])
    P.op("dve", lambda e: e.tensor_scalar(gsc[:, 0:32], gsc[:, 0:32], -1.0, None, ALU.mult), [gsc], [gsc])
    P.dma(gng[:], gng_d[:], writes=[gng])
    NP_ = 66
    Gb = P.sb("Gb", [128, 2 * NP_])
    Ga = P.sb("Ga", [128, 2 * NP_])
    gcum = P.sb("gcum", [128, 2 * NP_])
    eg = P.sb("eg", [128, 2 * NP_])
    etail = P.sb("etail", [128, 2 * NP_])
    glA = P.sb("glA", [128, 2 * NP_])
    glB = P.sb("glB", [128, 2 * NP_])
    nbeta = P.sb("nbeta", [128, 2 * NP_])
    beg = P.sb("beg", [128, 2 * NP_])
    STR = []
    for si in range(2):
        st_ = {}
        st_["S"] = P.sb("S%d" % si, [128, 128])
        st_["pt"] = {nm: [P.sb("pt_%s%d_%d" % (nm, i, si), [128, 128]) for i in range(2)] for nm in ("kT", "qT", "kt", "vt")}
        st_["G"] = {nm: P.sb("g_%s_%d" % (nm, si), [128, 128]) for nm in ("diag", "E1", "dT", "E2", "dS", "N", "Nt", "N2", "Nt2", "Tt", "Tt2", "vb", "kbeg",
                                                                           "u", "wT", "ktl", "inT", "vnew", "iv", "op")}
        bank_of = {0: 0, 1: 0, 11: 0, 12: 0, 15: 0, 2: 1, 3: 1, 6: 1, 9: 1, 14: 1, 4: 2, 7: 2, 10: 2, 13: 2, 5: 3, 8: 3}
        st_["Q"] = [Sub(PSB[4 * si + bank_of[j]], PSB[4 * si + bank_of[j]][:, 0:128]) for j in range(16)]
        STR.append(st_)

    def prep(hh, which, dst_T, dst_t):
        raw, cm, cv, sq = big
        row0 = COL_QKV + which * 2048 + hh * 128
        pbuf, pap = pd.rows(row0, row0 + 128)
        P.dma(raw[:], pap, reads=[pbuf], writes=[raw])
        P.op("pool", lambda e: e.tensor_copy(cm[:, 0:SEQ].rearrange("p (c r) -> p c r", r=128), raw[:, 0:SEQ].rearrange("p (r c) -> p c r", c=64)), [raw], [cm])
        P.op("pool", lambda e: e.tensor_copy(cm[:, SEQ:NT], raw[:, SEQ:NT]), [raw], [cm])
        wcol = (which * 16 + hh) * 4
        conv_seg(cv, cm, 0, SEQ, gcw, wcol, None)
        conv_seg(cv, cm, SEQ, NT, gcw, wcol, None)
        P.op("act", lambda e: e.activation(cv[:], cv[:], AF.Silu), [cv], [cv])
        if which < 2:
            P.op("act", lambda e: e.activation(sq[:], cv[:], AF.Square), [cv], [sq])
            for c in range(0, NT, 512):
                L = min(512, NT - c)
                pb = PSB[(c // 512) % 4]
                P.op("pe", lambda e, c=c, L=L, pb=pb: e.matmul(pb[:, 0:L], CS("ones"), sq[:, c:c + L], start=True, stop=True), [cst, sq], [pb])
                P.op("dve", lambda e, c=c, L=L, pb=pb: e.tensor_scalar(sq[:, c:c + L], pb[:, 0:L], EPS, None, ALU.add), [pb], [sq])
            P.op("act", lambda e: e.activation(sq[:], sq[:], AF.Sqrt), [sq], [sq])
            P.op("dve", lambda e: e.reciprocal(sq[:], sq[:]), [sq], [sq])
            scl = (128.0 ** -0.5) if which == 0 else 1.0
            P.op("dve", lambda e: e.scalar_tensor_tensor(cv[:], cv[:], scl, sq[:], ALU.mult, ALU.mult), [cv, sq], [cv])
        if dst_T is not None:
            P.dma(dst_T[:], cv[:], reads=[cv], writes=[dst_T])
        if dst_t is not None:
            for m in range(NP_):
                pb = PSB[4 + m % 4]
                P.op("pe", lambda e, m=m, pb=pb: e.transpose(pb[:, 0:128], cv[:, m * 128:(m + 1) * 128], CS("ident")), [cv, cst], [pb])
                if m % 2 == 0:
                    P.op("act", lambda e, m=m, pb=pb: e.activation(raw[:, (m % 8) * 128:(m % 8 + 1) * 128], pb[:, 0:128], AF.Copy), [pb], [raw])
                else:
                    P.op("dve", lambda e, m=m, pb=pb: e.tensor_copy(raw[:, (m % 8) * 128:(m % 8 + 1) * 128], pb[:, 0:128]), [pb], [raw])
                if m % 8 == 7 or m == NP_ - 1:
                    m0 = m - (m % 8)
                    cnt = m - m0 + 1
                    P.dma(dst_t[m0 * 128:(m + 1) * 128, :].rearrange("(a p) c -> p a c", p=128),
                          raw[:, 0:cnt * 128].rearrange("p (a c) -> p a c", c=128), reads=[raw], writes=[dst_t])

    def mm(out_buf, out_ap, l_ap, r_ap, rd):
        P.op("pe", lambda e: e.matmul(out_ap, l_ap, r_ap, start=True, stop=True), rd, [out_buf])

    ev_i = [0]

    def evac(dst_buf, dst_ap, src_buf, src_ap):
        ev_i[0] += 1
        if ev_i[0] % 2:
            P.op("act", lambda e: e.activation(dst_ap, src_ap, AF.Copy), [src_buf], [dst_buf])
        else:
            P.op("dve", lambda e: e.tensor_copy(dst_ap, src_ap), [src_buf], [dst_buf])

    for hh in range(DBG.get('nh', 16)):
        prep(hh, 0, qT_d, None)
        prep(hh, 1, kT_d, kt_d)
        prep(hh, 2, None, vt_d)
        for d in range(2):
            for (src_t, col) in ((Gb, COL_BETA), (Ga, COL_ALPHA)):
                rowi = col + d * 16 + hh
                pbuf, pap = pd.rows(rowi, rowi + 1)
                P.dma(src_t[:, d * NP_: d * NP_ + 64], pap[:, 0:SEQ].rearrange("o (i m) -> (o i) m", m=64), reads=[pbuf], writes=[src_t])
                P.dma(src_t[:, d * NP_ + 64: d * NP_ + 66], pap[:, SEQ:NT].rearrange("o (m i) -> (o i) m", i=128),
                      reads=[pbuf], writes=[src_t], allow_slow_non_contiguous=True)
        P.op("act", lambda e: e.activation(Gb[:], Gb[:], AF.Sigmoid), [Gb], [Gb])
        for d in range(2):
            sl = slice(d * NP_, (d + 1) * NP_)
            P.op("act", lambda e, sl=sl, d=d: e.activation(Ga[:, sl], Ga[:, sl], AF.Exp, bias=gsc[:, 32 + d * 16 + hh: 33 + d * 16 + hh]), [Ga, gsc], [Ga])
            P.op("dve", lambda e, sl=sl: e.tensor_scalar(Ga[:, sl], Ga[:, sl], 1.0, None, ALU.add), [Ga], [Ga])
            P.op("act", lambda e, sl=sl: e.activation(Ga[:, sl], Ga[:, sl], AF.Ln), [Ga], [Ga])
            P.op("dve", lambda e, sl=sl, d=d: e.tensor_scalar(Ga[:, sl], Ga[:, sl], gsc[:, d * 16 + hh: d * 16 + hh + 1], None, ALU.mult), [Ga, gsc], [Ga])
            pb = PSB[0]
            mm(pb, pb[:, 0:NP_], CS("ucum%d" % d), Ga[:, sl], [cst, Ga])
            evac(gcum, gcum[:, sl], pb, pb[:, 0:NP_])
            P.op("act", lambda e, sl=sl: e.activation(eg[:, sl], gcum[:, sl], AF.Exp), [gcum], [eg])
            pb = PSB[1]
            mm(pb, pb[:, 0:NP_], CS("bsame"), Ga[:, sl], [cst, Ga])
            P.op("dve", lambda e, sl=sl, pb=pb: e.tensor_tensor(etail[:, sl], pb[:, 0:NP_], gcum[:, sl], ALU.subtract), [pb, gcum], [etail])
            P.op("act", lambda e, sl=sl: e.activation(etail[:, sl], etail[:, sl], AF.Exp), [etail], [etail])
            pb = PSB[2]
            mm(pb, pb[:, 0:NP_], CS("selA"), Ga[:, sl], [cst, Ga])
            P.op("act", lambda e, sl=sl, pb=pb: e.activation(glA[:, sl], pb[:, 0:NP_], AF.Exp), [pb], [glA])
            pb = PSB[3]
            mm(pb, pb[:, 0:NP_], CS("selB"), Ga[:, sl], [cst, Ga])
            P.op("act", lambda e, sl=sl, pb=pb: e.activation(glB[:, sl], pb[:, 0:NP_], AF.Exp), [pb], [glB])
        P.op("dve", lambda e: e.tensor_scalar(nbeta[:], Gb[:], -1.0, None, ALU.mult), [Gb], [nbeta])
        P.op("dve", lambda e: e.tensor_tensor(beg[:], Gb[:], eg[:], ALU.mult), [Gb, eg], [beg])

        def evacA(dst_buf, dst_ap, src_buf, src_ap):
            P.op("act", lambda e: e.activation(dst_ap, src_ap, AF.Copy), [src_buf], [dst_buf])

        def chain(d, hh=hh):
            st_ = STR[d]
            S, pt, G, Q = st_["S"], st_["pt"], st_["G"], st_["Q"]
            li = 0
            P.op("dve", lambda e: e.memset(S[:], 0.0), [], [S])
            order = [64, 65] + list(range(64)) if d == 0 else [65, 64] + list(range(63, -1, -1))
            order = order[:DBG.get('npair', 99)]
            for m in order:
                col = d * NP_ + m
                lat = m < 64
                tl = {nm: pt[nm][li % 2] for nm in pt}
                li += 1
                P.dma(tl["kT"][:], kT_d[:, m * 128:(m + 1) * 128], reads=[kT_d], writes=[tl["kT"]])
                P.dma(tl["kt"][:], kt_d[m * 128:(m + 1) * 128, :], reads=[kt_d], writes=[tl["kt"]])
                P.dma(tl["vt"][:], vt_d[m * 128:(m + 1) * 128, :], reads=[vt_d], writes=[tl["vt"]])
                if lat:
                    P.dma(tl["qT"][:], qT_d[:, m * 128:(m + 1) * 128], reads=[qT_d], writes=[tl["qT"]])
                gc_col = gcum[:, col:col + 1]
                P.op("dve", lambda e: e.tensor_scalar(G["diag"][:], CS("ident"), gc_col, None, ALU.mult), [cst, gcum], [G["diag"]])
                pR = Q[0]
                mm(pR, pR[:], CS("ones"), G["diag"][:], [cst, G["diag"]])
                P.op("dve", lambda e: e.scalar_tensor_tensor(G["E2"][:], pR[:], gc_col, CS("pos%d" % d), ALU.subtract, ALU.add),
                     [pR, gcum, cst], [G["E2"]])
                P.op("act", lambda e: e.activation(G["dS"][:], G["E2"][:], AF.Exp, scale=-1.0), [G["E2"]], [G["dS"]])
                if lat:
                    P.op("dve", lambda e: e.scalar_tensor_tensor(G["E1"][:], pR[:], gc_col, CS("maskT%d" % d), ALU.subtract, ALU.add),
                         [pR, gcum, cst], [G["E1"]])
                    P.op("act", lambda e: e.activation(G["dT"][:], G["E1"][:], AF.Exp), [G["E1"]], [G["dT"]])
                yield
                pK = Q[1]
                mm(pK, pK[:], tl["kT"][:], tl["kT"][:], [tl["kT"]])
                P.op("dve", lambda e: e.scalar_tensor_tensor(G["N"][:], pK[:], nbeta[:, col:col + 1], G["dS"][:], ALU.mult, ALU.mult),
                     [pK, nbeta, G["dS"]], [G["N"]])
                pT = Q[2]
                P.op("pe", lambda e: e.transpose(pT[:], G["N"][:], CS("ident")), [G["N"], cst], [pT])
                evac(G["Nt"], G["Nt"][:], pT, pT[:])
                P.op("dve", lambda e: e.tensor_tensor(G["Tt"][:], G["Nt"][:], CS("ident"), ALU.add), [G["Nt"], cst], [G["Tt"]])
                yield
                Ncur, Ntcur, Nnx, Ntnx, Tcur, Tnx = "N", "Nt", "N2", "Nt2", "Tt", "Tt2"
                for lev in range(1, 6):
                    p1 = Q[3 + (lev % 2) * 3]
                    mm(p1, p1[:], G[Ntcur][:], G[Ncur][:], [G[Ntcur], G[Ncur]])
                    evac(G[Nnx], G[Nnx][:], p1, p1[:])
                    if lev < 5:
                        p2 = Q[4 + (lev % 2) * 3]
                        mm(p2, p2[:], G[Ncur][:], G[Ntcur][:], [G[Ntcur], G[Ncur]])
                        evac(G[Ntnx], G[Ntnx][:], p2, p2[:])
                    p3 = Q[5 + (lev % 2) * 3]
                    mm(p3, p3[:], G[Nnx][:], G[Tcur][:], [G[Nnx], G[Tcur]])
                    P.op("dve", lambda e: e.tensor_tensor(G[Tnx][:], p3[:], G[Tcur][:], ALU.add), [p3, G[Tcur]], [G[Tnx]])
                    Ncur, Nnx = Nnx, Ncur
                    Ntcur, Ntnx = Ntnx, Ntcur
                    Tcur, Tnx = Tnx, Tcur
                    yield
                TT = G[Tcur]
                P.op("act", lambda e: e.activation(G["vb"][:], tl["vt"][:], AF.Copy, scale=Gb[:, col:col + 1]), [tl["vt"], Gb], [G["vb"]])
                P.op("act", lambda e: e.activation(G["kbeg"][:], tl["kt"][:], AF.Copy, scale=beg[:, col:col + 1]), [tl["kt"], beg], [G["kbeg"]])
                P.op("dve", lambda e: e.tensor_scalar(G["ktl"][:], tl["kt"][:], etail[:, col:col + 1], None, ALU.mult), [tl["kt"], etail], [G["ktl"]])
                pU = Q[9]
                mm(pU, pU[:], TT[:], G["vb"][:], [TT, G["vb"]])
                evac(G["u"], G["u"][:], pU, pU[:])
                pW = Q[10]
                mm(pW, pW[:], G["kbeg"][:], TT[:], [TT, G["kbeg"]])
                evac(G["wT"], G["wT"][:], pW, pW[:])
                if lat:
                    pQ = Q[11]
                    mm(pQ, pQ[:], tl["kT"][:], tl["qT"][:], [tl["kT"], tl["qT"]])
                    P.op("dve", lambda e: e.tensor_tensor(G["inT"][:], pQ[:], G["dT"][:], ALU.mult), [pQ, G["dT"]], [G["inT"]])
                yield
                halves = [(0, 64, glA), (64, 128, glB)]
                if d == 1:
                    halves = halves[::-1]
                for (a, b, gl) in halves:
                    pws = Q[12]
                    mm(pws, pws[:], G["wT"][:], S[:], [G["wT"], S])
                    P.op("dve", lambda e: e.tensor_tensor(G["vnew"][a:b, :], G["u"][a:b, :], pws[a:b, :], ALU.subtract),
                         [G["u"], pws], [G["vnew"]])
                    if lat:
                        pqs = Q[13]
                        mm(pqs, pqs[:], tl["qT"][:], S[:], [tl["qT"], S])
                        piv = Q[14]
                        mm(piv, piv[:], G["inT"][a:b, :], G["vnew"][a:b, :], [G["inT"], G["vnew"]])
                        P.op("act", lambda e: e.activation(G["iv"][a:b, :], piv[a:b, :], AF.Copy), [piv], [G["iv"]])
                        P.op("dve", lambda e: e.scalar_tensor_tensor(G["op"][a:b, :], pqs[a:b, :], eg[a:b, col:col + 1], G["iv"][a:b, :], ALU.mult, ALU.add),
                             [pqs, eg, G["iv"]], [G["op"]])
                    pds = Q[15]
                    mm(pds, pds[:], G["ktl"][a:b, :], G["vnew"][a:b, :], [G["ktl"], G["vnew"]])
                    P.op("dve", lambda e: e.scalar_tensor_tensor(S[:], S[:], gl[:, col:col + 1], pds[:], ALU.mult, ALU.add),
                         [S, gl, pds], [S])
                    yield
                if lat:
                    first = (m <= 31) if d == 0 else (m >= 32)
                    po = Q[2] if first else Q[1]
                    P.op("pe", lambda e: e.transpose(po[:], G["op"][:], CS("ident")), [G["op"], cst], [po])
                    if first:
                        P.op("act", lambda e: e.activation(big[0][:, m:SEQ:64], po[:], AF.Copy), [po], [big[0]])
                    else:
                        P.op("dve", lambda e: e.tensor_tensor(big[0][:, m:SEQ:64], po[:], big[0][:, m:SEQ:64], ALU.add), [po, big[0]], [big[0]])
                yield

        P.fence()
        gens = [chain(0), chain(1)]
        alive = [True, True]
        while any(alive):
            for gi in range(2):
                if alive[gi]:
                    try:
                        next(gens[gi])
                    except StopIteration:
                        alive[gi] = False
        P.fence()
        oT, zT, sq2, _ = big
        pbuf, pap = pd.rows(COL_Z + hh * 128, COL_Z + (hh + 1) * 128)
        P.dma(zT[:, 0:SEQ], pap[:, 0:SEQ], reads=[pbuf], writes=[zT])
        P.op("act", lambda e: e.activation(zT[:, 0:SEQ], zT[:, 0:SEQ], AF.Silu), [zT], [zT])
        P.op("act", lambda e: e.activation(sq2[:, 0:SEQ], oT[:, 0:SEQ], AF.Square), [oT], [sq2])
        for c in range(0, SEQ, 512):
            pb = PSB[(c // 512) % 4]
            P.op("pe", lambda e, c=c, pb=pb: e.matmul(pb[:, 0:512], CS("ones"), sq2[:, c:c + 512], start=True, stop=True), [cst, sq2], [pb])
            P.op("dve", lambda e, c=c, pb=pb: e.tensor_scalar(sq2[:, c:c + 512], pb[:, 0:512], 1.0 / 128, EPS, ALU.mult, ALU.add), [pb], [sq2])
        P.op("act", lambda e: e.activation(sq2[:, 0:SEQ], sq2[:, 0:SEQ], AF.Sqrt), [sq2], [sq2])
        P.op("dve", lambda e: e.reciprocal(sq2[:, 0:SEQ], sq2[:, 0:SEQ]), [sq2], [sq2])
        P.op("dve", lambda e: e.scalar_tensor_tensor(oT[:, 0:SEQ], oT[:, 0:SEQ], gng[:, 0:1], sq2[:, 0:SEQ], ALU.mult, ALU.mult), [oT, gng, sq2], [oT])
        P.op("pool", lambda e: e.tensor_tensor(oT[:, 0:SEQ], oT[:, 0:SEQ], zT[:, 0:SEQ], ALU.mult), [oT, zT], [oT])
        P.dma(y_v[2048 + hh * 128: 2048 + (hh + 1) * 128, :, :], oT[:, 0:SEQ].rearrange("p (a t) -> p a t", t=256), reads=[oT], writes=[y_d])

    if stop_after < 4:
        P.emit(final_wait_bufs=[mod_d, p_bufs[2], y_d])
        return nc
    P.barrier()
    TB = 2
    T = TB * 128
    gprod_d = P.dram("gprod_s", [2, D], F32)
    R1 = P.sb("R1", [128, 2 * D])
    R2 = P.sb("R2", [128, 2 * D])
    gmul = P.sb("gmul", [128, D])
    P.dma(R1[:, 0:D], grow_d[0:1, :].partition_broadcast(128), writes=[R1])
    P.dma(R1[:, D:2 * D], grow_d[1:2, :].partition_broadcast(128), writes=[R1])
    P.dma(R2[:, 0:D], mod_d[0:1, 2 * D:3 * D].partition_broadcast(128), reads=[mod_d], writes=[R2])
    P.dma(R2[:, D:2 * D], mod_d[0:1, 5 * D:6 * D].partition_broadcast(128), reads=[mod_d], writes=[R2])
    P.op("pool", lambda e: e.tensor_tensor(R1[:], R1[:], R2[:], ALU.mult), [R1, R2], [R1])
    P.dma(gprod_d[0:1, :], R1[0:1, 0:D], reads=[R1], writes=[gprod_d])
    P.dma(gprod_d[1:2, :], R1[0:1, D:2 * D], reads=[R1], writes=[gprod_d])
    Mb = P.sb("Mb", [128, 32, T], BF16)
    PR["xn"] = Buf(Mb[:].rearrange("p a b -> p (a b)").rearrange("p (a b) -> p a b", b=D), "xnv")
    Mb.alias = PR["xn"]
    hT = [P.sb("hT%d" % k, [128, T], BF16) for k in range(32)]
    PR["hT"] = hT
    aT = [P.sb("aT%d" % j, [128, T], BF16) for j in range(86)]
    rbc = P.sb("rbc", [128, T])
    sqt = P.sb("sqt", [128, T])
    sil = P.sb("sil", [128, T])
    wsl = [P.sb("wsl%d" % i, [128, 2048]) for i in range(2)]
    wslb = [P.sb("wslb%d" % i, [128, 2048], BF16) for i in range(2)]
    cast_i = [0]

    def cast(dst_buf, dst_ap, src_buf, src_ap):
        cast_i[0] += 1
        eng = ("pool", "pool", "dve")[cast_i[0] % 3]
        P.op(eng, lambda e: e.tensor_copy(dst_ap, src_ap), [src_buf], [dst_buf])

    def tok_epilogue(RB, tb, which):
        osl = RB[:, tb * D:(tb + 1) * D]
        P.op("act", lambda e: e.activation(gmul[:], osl, AF.Square, accum_out=st2[:, tb:tb + 1]), [RB], [gmul, st2])
        rstd_from_ss(st2[:, tb:tb + 1], st2[:, 4 + tb:5 + tb], D, [st2], [st2])
        P.dma(gmul[:], gprod_d[which:which + 1, :].partition_broadcast(128), reads=[gprod_d], writes=[gmul])
        P.op("dve", lambda e: e.scalar_tensor_tensor(osl, osl, st2[:, 4 + tb:5 + tb], gmul[:], ALU.mult, ALU.mult), [RB, st2, gmul], [RB])
        return osl

    wj = 0
    MX = [Mb]
    yidx = P.sb("yidx", [128, 256], mybir.dt.uint32)
    P.dma(yidx[:], yidx_d[:], writes=[yidx])
    for ti in range(DBG.get('ntile', TOKQ // T)):
        t0 = ti * T
        ymix = R1
        for k in range(32):
            P.op("pool", lambda e, k=k, ti=ti: e.indirect_dma_start(
                out=ymix[:, k * T:(k + 1) * T], out_offset=None, in_=y_d[:],
                in_offset=bass.IndirectOffsetOnAxis(ap=yidx[:, ti * 32 + k: ti * 32 + k + 1], axis=0)),
                [y_d, yidx], [ymix], is_dma=True)
        pss = PSB[0]
        for k in range(16):
            P.op("act", lambda e, k=k: e.activation(sqt[:], ymix[:, k * T:(k + 1) * T], AF.Square), [ymix], [sqt])
            P.op("pe", lambda e, k=k: e.matmul(pss[:, 0:T], CS("ones"), sqt[:], start=(k == 0), stop=(k == 15)), [cst, sqt], [pss])
        P.op("dve", lambda e: e.tensor_scalar(rbc[:], pss[:, 0:T], 1.0 / 2048, EPS, ALU.mult, ALU.add), [pss], [rbc])
        P.op("act", lambda e: e.activation(rbc[:], rbc[:], AF.Sqrt), [rbc], [rbc])
        P.op("dve", lambda e: e.reciprocal(rbc[:], rbc[:]), [rbc], [rbc])
        for k in range(32):
            if k < 16:
                P.op("dve", lambda e, k=k: e.scalar_tensor_tensor(Mb[:, k, :], ymix[:, k * T:(k + 1) * T], lng[:, k:k + 1], rbc[:], ALU.mult, ALU.mult),
                     [ymix, lng, rbc, PR["xn"]], [Mb, PR["xn"]])
            else:
                P.op("pool", lambda e, k=k: e.tensor_copy(Mb[:, k, :], ymix[:, k * T:(k + 1) * T]), [ymix, PR["xn"]], [Mb, PR["xn"]])
        for nh in range(2):
            for k in range(32):
                ws, wb = wsl[wj % 2], wslb[wj % 2]
                wj += 1
                P.dma(ws[:], wout_d[k * 128:(k + 1) * 128, nh * 2048:(nh + 1) * 2048], writes=[ws])
                cast(wb, wb[:], ws, ws[:])
                for tb in range(TB):
                    for n4 in range(4):
                        pb = PSB[tb * 4 + n4]
                        P.op("pe", lambda e, k=k, tb=tb, n4=n4, pb=pb, wb=wb: e.matmul(pb[:, :], Mb[:, k, tb * 128:(tb + 1) * 128], wb[:, n4 * 512:(n4 + 1) * 512],
                                                                                   start=(k == 0), stop=(k == 31)), [Mb, wb], [pb])
            for tb in range(TB):
                for n4 in range(4):
                    pb = PSB[tb * 4 + n4]
                    c0 = tb * D + nh * 2048 + n4 * 512
                    evac(R2, R2[:, c0:c0 + 512], pb, pb[:, :])
        for tb in range(TB):
            osl = tok_epilogue(R2, tb, 0)
            xs = R1[:, tb * D:(tb + 1) * D]
            P.dma(xs, xq_d[t0 + tb * 128: t0 + (tb + 1) * 128, :], writes=[R1])
            P.op("pool", lambda e, xs=xs, osl=osl: e.tensor_tensor(osl, osl, xs, ALU.add), [R2, R1], [R2])
        for tb in range(TB):
            xn = PR["xn"]
            src_ap = R2[:, tb * D:(tb + 1) * D]
            P.op("act", lambda e, tb=tb, src_ap=src_ap: e.activation(xn[:, tb, :], src_ap, AF.Square, accum_out=st1[:, tb:tb + 1]), [R2, Mb], [xn, Mb, st1])
            rstd_from_ss(st1[:, tb:tb + 1], st1[:, 4 + tb:5 + tb], D, [st1], [st1])
            P.op("dve", lambda e, tb=tb, src_ap=src_ap: e.tensor_scalar(xn[:, tb, :], src_ap, st1[:, 4 + tb:5 + tb], None, ALU.mult), [R2, st1, Mb], [xn, Mb])
        make_hT(TB, 0, 1)
        for j in range(86):
            pg, pu = PSB[(2 * j) % 8], PSB[(2 * j + 1) % 8]
            for kh in range(2):
                for (wsrc, pacc, si) in ((wg_d, pg, 0), (wu_d, pu, 1)):
                    ws, wb = wsl[si], wslb[si]
                    wsv = ws[:].rearrange("p (k n) -> p k n", n=128)
                    wbv = wb[:].rearrange("p (k n) -> p k n", n=128)
                    P.dma(wsv, wsrc[kh * 2048:(kh + 1) * 2048, j * 128:(j + 1) * 128].rearrange("(k p) n -> p k n", p=128), writes=[ws])
                    cast(wb, wb[:], ws, ws[:])
                    for kk in range(16):
                        k = kh * 16 + kk
                        P.op("pe", lambda e, k=k, kk=kk, pacc=pacc, wbv=wbv: e.matmul(pacc[:, 0:T], wbv[:, kk, :], hT[k][:, 0:T], start=(k == 0), stop=(k == 31)),
                             [wb, hT[k]], [pacc])
            P.op("act", lambda e, pg=pg: e.activation(sil[:], pg[:, 0:T], AF.Silu), [pg], [sil])
            P.op("dve", lambda e, j=j, pu=pu: e.tensor_tensor(aT[j][:], pu[:, 0:T], sil[:], ALU.mult), [pu, sil], [aT[j]])
        for nh in range(2):
            for k in range(86):
                ws, wb = wsl[wj % 2], wslb[wj % 2]
                wj += 1
                P.dma(ws[:], wd_d[k * 128:(k + 1) * 128, nh * 2048:(nh + 1) * 2048], writes=[ws])
                cast(wb, wb[:], ws, ws[:])
                for tb in range(TB):
                    for n4 in range(4):
                        pb = PSB[tb * 4 + n4]
                        P.op("pe", lambda e, k=k, tb=tb, n4=n4, pb=pb, wb=wb: e.matmul(pb[:, :], aT[k][:, tb * 128:(tb + 1) * 128], wb[:, n4 * 512:(n4 + 1) * 512],
                                                                                   start=(k == 0), stop=(k == 85)), [aT[k], wb], [pb])
            for tb in range(TB):
                for n4 in range(4):
                    pb = PSB[tb * 4 + n4]
                    c0 = tb * D + nh * 2048 + n4 * 512
                    evac(R1, R1[:, c0:c0 + 512], pb, pb[:, :])
        for tb in range(TB):
            osl = tok_epilogue(R1, tb, 1)
            P.op("pool", lambda e, tb=tb, osl=osl: e.tensor_tensor(osl, osl, R2[:, tb * D:(tb + 1) * D], ALU.add), [R1, R2], [R1])
            P.dma(out_d[t0 + tb * 128: t0 + (tb + 1) * 128, :], osl, reads=[R1], writes=[out_d])

    P.emit(final_wait_bufs=[out_d] + ([mod_d, p_bufs[2], y_d] if dbg else []))
    return nc


def kernel(**inp):
    f = lambda a: np.ascontiguousarray(np.asarray(a, dtype=np.float32))
    if "_dbg" in inp:
        globals()["NCORES"] = 1
    x = f(inp["x"]); c = f(inp["c"]); ctx = f(inp["ctx"]); c_ctx = f(inp["c_ctx"])
    L = 0
    gvecs = np.concatenate([f(inp["g_pre_mix"])[L].reshape(32, 128).T, f(inp["g_pre_ffn"])[L].reshape(32, 128).T], axis=1)
    grows = np.stack([f(inp["g_post_mix"])[L], f(inp["g_post_ffn"])[L]], 0)
    lcw = np.concatenate([f(inp["lru_conv_w"])[L], f(inp["lru_conv_b"])[L][None]], 0)
    lcw = lcw.reshape(5, 16, 128).transpose(2, 1, 0).reshape(128, 80)
    lv = np.stack([f(inp["lru_b_a"])[L], f(inp["lru_b_x"])[L], f(inp["lru_lambda"])[L]], -1)
    lv = lv.reshape(2, 16, 128, 3).transpose(2, 0, 1, 3).reshape(128, 96)
    lng = f(inp["lru_norm_g"])[L].reshape(16, 128).T
    gcw = f(inp["gdn_conv_w"])[L].reshape(4, 48, 128).transpose(2, 1, 0).reshape(128, 192)
    gsc = np.concatenate([f(inp["gdn_a_log"])[L].reshape(-1), f(inp["gdn_dt_bias"])[L].reshape(-1)])[None]
    gng = f(inp["gdn_norm_g"])[L].reshape(128, 1)
    common = {
        "w_ada": f(inp["w_ada"])[L], "b_ada": f(inp["b_ada"])[L][None] if f(inp["b_ada"]).ndim == 2 else f(inp["b_ada"]),
        "gvecs": f(gvecs), "grows": f(grows), "w_in": f(inp["w_in"])[L], "lru_cw": f(lcw),
        "lru_w_a": f(inp["lru_w_a"])[L].reshape(2 * 16 * 128, 128), "lru_w_x": f(inp["lru_w_x"])[L].reshape(2 * 16 * 128, 128),
        "lru_vecs": f(lv), "lru_ng": f(lng), "gdn_cw": f(gcw), "gdn_sc": f(gsc), "gdn_ng": f(gng),
        "w_out": f(inp["w_out"])[L], "w_g": f(inp["w_ffn_gate"])[L], "w_u": f(inp["w_ffn_up"])[L], "w_d": f(inp["w_ffn_down"])[L],
        "consts": CONST_ARR,
    }
    common["b_ada"] = f(inp["b_ada"])[L].reshape(1, -1)
    in_maps = []
    for core in range(NCORES):
        b, q = (core // 4, core % 4) if NCORES == 8 else (core, 0)
        m = dict(common)
        m["x"] = x[b]
        m["ctx"] = ctx[b]
        m["cin"] = f(np.concatenate([c[b].reshape(32, 128), c_ctx.reshape(32, 128)], 0))
        m["x_q"] = np.ascontiguousarray(x[b, q * TOKQ:(q + 1) * TOKQ])
        ti = np.arange(8)[None, :, None]; kk = np.arange(32)[None, None, :]; pp = np.arange(128)[:, None, None]
        m["yidx"] = np.ascontiguousarray(((q * 8 + ti) * D + kk * 128 + pp).reshape(128, 256).astype(np.uint32))
        in_maps.append(m)
    if "_dbg" in inp:
        nc = build(stop_after=inp["_dbg"], dbg=True)
        return run_bass_kernel_spmd(nc, in_maps, core_ids=list(range(NCORES)))
    nc = build()
    res = run_bass_kernel_spmd(nc, in_maps, core_ids=list(range(NCORES)))
    out = np.stack([res.results[c_]["out"] for c_ in range(NCORES)], 0).reshape(2, SEQ, D)
    return out.astype(np.float32)
```
